# Optimizing a Trainium2 kernel written in Bass

```python
import math
import jax, jax.numpy as jnp
from jax import lax
import numpy as np

D_MODEL = 2048
BATCH = 1
SEQ = 16384
DEPTH = 4

N_MEM = 256
NSA_HEADS = 16
NSA_GROUPS = 4
NSA_HPG = NSA_HEADS // NSA_GROUPS
HEAD_DIM = 64
NSA_WIDTH = NSA_HEADS * HEAD_DIM
KV_WIDTH = NSA_GROUPS * HEAD_DIM
CMP_LEN = 32
CMP_STRIDE = 16
CMP_HID = 2 * HEAD_DIM
SEL_BLK = 64
SEL_TOPN = 16
WIN = 512
Q_BLK = 128
FORCE_SCORE = 1e4
POOL_WINDOWS = (2, 4, 8, 16)
POOL_GROUPS = 4
POOL_WIDTH = D_MODEL // 2
POOL_GW = POOL_WIDTH // POOL_GROUPS
REL_BUCKETS = 32
REL_MAX_DIST = 2048
X_HEADS = 4
X_HEAD_DIM = 128
X_WIDTH = X_HEADS * X_HEAD_DIM
D_FF = -(-8 * D_MODEL // (3 * 256)) * 256
IN_WIDTH = NSA_WIDTH + 6 * KV_WIDTH + 3 * NSA_HEADS + POOL_WIDTH + 2 * D_MODEL

kernel_name = "nsa_pool_hybrid_block"


def rms_norm(x, g, eps=1e-6):
    xf = x.astype(jnp.float32)
    y = xf * lax.rsqrt(jnp.mean(xf * xf, axis=-1, keepdims=True) + eps)
    return (y * g.astype(jnp.float32)).astype(x.dtype)


def rel_bucket(dist):
    n = jnp.maximum(dist, 0)
    exact = REL_BUCKETS // 2
    nf = jnp.maximum(n, 1).astype(jnp.float32)
    large = exact + (jnp.log(nf / exact) / math.log(REL_MAX_DIST / exact)
                     * (REL_BUCKETS - exact)).astype(jnp.int32)
    return jnp.where(n < exact, n, jnp.minimum(large, REL_BUCKETS - 1))


def masked_softmax(s, mask):
    s = jnp.where(mask, s.astype(jnp.float32), -1e30)
    m = jnp.max(s, axis=-1, keepdims=True)
    e = jnp.where(mask, jnp.exp(s - m), 0.0)
    return e / jnp.maximum(jnp.sum(e, axis=-1, keepdims=True), 1.0)


def compress_kv(raw, pe, w1, w2):
    b, s = raw.shape[0], raw.shape[1]
    n_cmp = (s - CMP_LEN) // CMP_STRIDE + 1
    idx = jnp.arange(n_cmp)[:, None] * CMP_STRIDE + jnp.arange(CMP_LEN)[None, :]
    blk = raw[:, idx] + pe[:, None, :]
    flat = jnp.transpose(blk, (0, 1, 3, 2, 4)).reshape(b, n_cmp, NSA_GROUPS, CMP_LEN * HEAD_DIM)
    return jax.nn.gelu(flat @ w1) @ w2


def cmp_to_sel_overlap(n_cmp, n_sel):
    i = jnp.arange(n_cmp)[:, None]
    j = jnp.arange(n_sel)[None, :]
    lo = jnp.maximum(i * CMP_STRIDE, j * SEL_BLK)
    hi = jnp.minimum(i * CMP_STRIDE + CMP_LEN, (j + 1) * SEL_BLK)
    return jnp.maximum(hi - lo, 0).astype(jnp.float32) / CMP_LEN


def nsa_attention(q, k_cmp_raw, v_cmp_raw, k_sel, v_sel, k_win, v_win, gate_logits,
                  cmp_pe, cmp_w1, cmp_w2, rel_bias):
    b, s = q.shape[0], q.shape[1]
    scale = HEAD_DIM ** -0.5
    kc = compress_kv(k_cmp_raw, cmp_pe[0], cmp_w1[0], cmp_w2[0])
    vc = compress_kv(v_cmp_raw, cmp_pe[1], cmp_w1[1], cmp_w2[1])
    n_cmp = kc.shape[1]
    n_sel = s // SEL_BLK
    topn = min(SEL_TOPN, n_sel)
    overlap = cmp_to_sel_overlap(n_cmp, n_sel)
    cmp_end = jnp.arange(n_cmp) * CMP_STRIDE + CMP_LEN - 1
    kb = k_sel.reshape(b, n_sel, SEL_BLK, NSA_GROUPS, HEAD_DIM).transpose(0, 3, 1, 2, 4)
    vb = v_sel.reshape(b, n_sel, SEL_BLK, NSA_GROUPS, HEAD_DIM).transpose(0, 3, 1, 2, 4)
    kw_pad = jnp.pad(k_win, ((0, 0), (WIN, 0), (0, 0), (0, 0)))
    vw_pad = jnp.pad(v_win, ((0, 0), (WIN, 0), (0, 0), (0, 0)))
    gates = jax.nn.sigmoid(gate_logits.astype(jnp.float32)).astype(q.dtype).reshape(b, s, NSA_HEADS, 3)
    tab_g = rel_bias.reshape(NSA_GROUPS, NSA_HPG, REL_BUCKETS)
    gather_blocks = jax.vmap(jax.vmap(lambda blocks, idx: blocks[idx]))
    group_bias = jax.vmap(jax.vmap(lambda tab, bk: tab[:, bk]), in_axes=(None, 0))

    def head_bias(dist):
        return tab_g[:, :, rel_bucket(dist)]

    def block(qi):
        q0 = qi * Q_BLK
        t = q0 + jnp.arange(Q_BLK)
        qg = lax.dynamic_slice_in_dim(q, q0, Q_BLK, 1).reshape(b, Q_BLK, NSA_GROUPS, NSA_HPG, HEAD_DIM)
        gq = lax.dynamic_slice_in_dim(gates, q0, Q_BLK, 1)
        d_c = t[:, None] - cmp_end[None, :]
        s_c = jnp.einsum('bqghd,bngd->bghqn', qg, kc).astype(jnp.float32) * scale + head_bias(d_c)
        p_c = masked_softmax(s_c, d_c >= 0)
        o_c = jnp.einsum('bghqn,bngd->bqghd', p_c.astype(vc.dtype), vc)
        imp = jnp.einsum('bghqn,nj->bgqj', p_c, overlap)
        cur = t // SEL_BLK
        j = jnp.arange(n_sel)[None, :]
        forced = (j == 0) | (j == cur[:, None]) | (j == cur[:, None] - 1)
        imp = jnp.where(forced, FORCE_SCORE, imp)
        imp = jnp.where(j > cur[:, None], -1.0, imp)
        top_val, top_idx = lax.top_k(imp, topn)
        ks = gather_blocks(kb, top_idx)
        vs = gather_blocks(vb, top_idx)
        pos = top_idx[..., None] * SEL_BLK + jnp.arange(SEL_BLK)
        d_s = t[:, None, None] - pos
        mask_s = (top_val >= 0)[..., None] & (d_s >= 0)
        bias_s = group_bias(tab_g, rel_bucket(d_s))
        s_s = jnp.einsum('bqghd,bgqnkd->bghqnk', qg, ks).astype(jnp.float32) * scale + bias_s
        m_len = topn * SEL_BLK
        p_s = masked_softmax(s_s.reshape(b, NSA_GROUPS, NSA_HPG, Q_BLK, m_len),
                             mask_s.reshape(b, NSA_GROUPS, 1, Q_BLK, m_len))
        o_s = jnp.einsum('bghqm,bgqmd->bqghd', p_s.astype(vs.dtype),
                         vs.reshape(b, NSA_GROUPS, Q_BLK, m_len, HEAD_DIM))
        kw = lax.dynamic_slice_in_dim(kw_pad, q0, WIN + Q_BLK, 1)
        vw = lax.dynamic_slice_in_dim(vw_pad, q0, WIN + Q_BLK, 1)
        s_pos = q0 - WIN + jnp.arange(WIN + Q_BLK)
        d_w = t[:, None] - s_pos[None, :]
        mask_w = (d_w >= 0) & (d_w < WIN) & (s_pos[None, :] >= 0)
        s_w = jnp.einsum('bqghd,bkgd->bghqk', qg, kw).astype(jnp.float32) * scale + head_bias(d_w)
        p_w = masked_softmax(s_w, mask_w)
        o_w = jnp.einsum('bghqk,bkgd->bqghd', p_w.astype(vw.dtype), vw)
        o = jnp.stack([o_c, o_s, o_w], axis=-1).reshape(b, Q_BLK, NSA_HEADS, HEAD_DIM, 3)
        return jnp.einsum('bqhdc,bqhc->bqhd', o, gq)

    out = lax.map(block, jnp.arange(s // Q_BLK))
    return jnp.transpose(out, (1, 0, 2, 3, 4)).reshape(b, s, NSA_WIDTH)


def pool_mixer(u, w_pool, pool_scale):
    b, s, _ = u.shape
    uf = u.astype(jnp.float32)
    cnt_base = jnp.arange(s) + 1
    outs = []
    for gi, w in enumerate(POOL_WINDOWS):
        ug = uf[..., gi * POOL_GW:(gi + 1) * POOL_GW]
        c = jnp.concatenate([jnp.zeros((b, 1, POOL_GW), jnp.float32), jnp.cumsum(ug, axis=1)], axis=1)
        hi = c[:, 1:]
        lo = jnp.pad(c[:, :s + 1 - w], ((0, 0), (w - 1, 0), (0, 0)))
        cnt = jnp.minimum(cnt_base, w).astype(jnp.float32)[None, :, None]
        outs.append((hi - lo) / cnt - ug)
    y = jnp.stack(outs, axis=2).astype(u.dtype)
    y = jnp.einsum('bsgc,gcd->bsgd', y, w_pool)
    return y.reshape(b, s, POOL_WIDTH) * pool_scale


def memory_cross_attention(h, mem_n, wq, wkv, wo):
    b, s, _ = h.shape
    q = (h @ wq).reshape(b, s, X_HEADS, X_HEAD_DIM)
    k, v = jnp.split(mem_n @ wkv, 2, axis=-1)
    k = k.reshape(b, -1, X_HEADS, X_HEAD_DIM)
    v = v.reshape(b, -1, X_HEADS, X_HEAD_DIM)
    sc = jnp.einsum('bshd,bmhd->bhsm', q, k).astype(jnp.float32) * (X_HEAD_DIM ** -0.5)
    p = jax.nn.softmax(sc, axis=-1).astype(v.dtype)
    o = jnp.einsum('bhsm,bmhd->bshd', p, v).reshape(b, s, X_WIDTH)
    return o @ wo


def setup_inputs(seed: int = 0) -> dict:
    key = jax.random.key(seed)
    ks = jax.random.split(key, 32)
    f32 = jnp.float32

    def w(k, shape, fan_in):
        return jax.random.normal(k, shape, f32) * (fan_in ** -0.5)

    def gain(k, shape):
        return 1.0 + 0.05 * jax.random.normal(k, shape, f32)

    L = DEPTH
    return {
        "x": jax.random.normal(ks[0], (BATCH, SEQ, D_MODEL), f32),
        "mem": jax.random.normal(ks[1], (BATCH, N_MEM, D_MODEL), f32),
        "rel_bias": 0.5 * jax.random.normal(ks[2], (NSA_HEADS, REL_BUCKETS), f32),
        "ln_mix_pre": gain(ks[3], (L, D_MODEL)),
        "ln_mix_post": gain(ks[4], (L, D_MODEL)),
        "ln_x_pre": gain(ks[5], (L, D_MODEL)),
        "ln_x_post": gain(ks[6], (L, D_MODEL)),
        "ln_mem": gain(ks[7], (L, D_MODEL)),
        "ln_ffn_pre": gain(ks[8], (L, D_MODEL)),
        "ln_ffn_post": gain(ks[9], (L, D_MODEL)),
        "w_in": w(ks[10], (L, D_MODEL, IN_WIDTH), D_MODEL),
        "cmp_pe": 0.1 * jax.random.normal(ks[11], (L, 2, CMP_LEN, HEAD_DIM), f32),
        "cmp_w1": w(ks[12], (L, 2, CMP_LEN * HEAD_DIM, CMP_HID), CMP_LEN * HEAD_DIM),
        "cmp_w2": w(ks[13], (L, 2, CMP_HID, HEAD_DIM), CMP_HID),
        "w_pool": w(ks[14], (L, POOL_GROUPS, POOL_GW, POOL_GW), POOL_GW),
        "pool_scale": gain(ks[15], (L, POOL_WIDTH)),
        "w_br_attn": w(ks[16], (L, NSA_WIDTH, D_MODEL), NSA_WIDTH),
        "w_br_pool": w(ks[17], (L, POOL_WIDTH, D_MODEL), POOL_WIDTH),
        "w_mix_out": w(ks[18], (L, D_MODEL, D_MODEL), D_MODEL),
        "w_xq": w(ks[19], (L, D_MODEL, X_WIDTH), D_MODEL),
        "w_xkv": w(ks[20], (L, D_MODEL, 2 * X_WIDTH), D_MODEL),
        "w_xo": w(ks[21], (L, X_WIDTH, D_MODEL), X_WIDTH),
        "w_gate": w(ks[22], (L, D_MODEL, D_FF), D_MODEL),
        "w_up": w(ks[23], (L, D_MODEL, D_FF), D_MODEL),
        "w_down": w(ks[24], (L, D_FF, D_MODEL), D_FF),
    }


def reference(x, mem, rel_bias, ln_mix_pre, ln_mix_post, ln_x_pre, ln_x_post, ln_mem,
              ln_ffn_pre, ln_ffn_post, w_in, cmp_pe, cmp_w1, cmp_w2, w_pool, pool_scale,
              w_br_attn, w_br_pool, w_mix_out, w_xq, w_xkv, w_xo, w_gate, w_up, w_down):
    b, s, _ = x.shape
    splits = [NSA_WIDTH,
              NSA_WIDTH + 6 * KV_WIDTH,
              NSA_WIDTH + 6 * KV_WIDTH + 3 * NSA_HEADS,
              NSA_WIDTH + 6 * KV_WIDTH + 3 * NSA_HEADS + POOL_WIDTH]
    for l in range(DEPTH):
        h = rms_norm(x, ln_mix_pre[l])
        z = h @ w_in[l]
        q, kv, nsa_gl, u, merge_logits = jnp.split(z, splits, axis=-1)
        q = q.reshape(b, s, NSA_HEADS, HEAD_DIM)
        kv = kv.reshape(b, s, 6, NSA_GROUPS, HEAD_DIM)
        a = nsa_attention(q, kv[:, :, 0], kv[:, :, 1], kv[:, :, 2], kv[:, :, 3],
                          kv[:, :, 4], kv[:, :, 5], nsa_gl,
                          cmp_pe[l], cmp_w1[l], cmp_w2[l], rel_bias)
        p = pool_mixer(u, w_pool[l], pool_scale[l])
        g_a, g_p = jnp.split(jax.nn.sigmoid(merge_logits.astype(jnp.float32)).astype(x.dtype), 2, axis=-1)
        y = (g_a * (a @ w_br_attn[l]) + g_p * (p @ w_br_pool[l])) @ w_mix_out[l]
        x = x + rms_norm(y, ln_mix_post[l])
        h = rms_norm(x, ln_x_pre[l])
        mem_n = rms_norm(mem, ln_mem[l])
        y = memory_cross_attention(h, mem_n, w_xq[l], w_xkv[l], w_xo[l])
        x = x + rms_norm(y, ln_x_post[l])
        h = rms_norm(x, ln_ffn_pre[l])
        y = (jax.nn.silu(h @ w_gate[l]) * (h @ w_up[l])) @ w_down[l]
        x = x + rms_norm(y, ln_ffn_post[l])
    return x
```

```python
import numpy as np
import ml_dtypes
from contextlib import ExitStack
import concourse.bass as bass
import concourse.mybir as mybir
from concourse.bass_utils import run_bass_kernel_spmd

F32 = mybir.dt.float32
BF16 = mybir.dt.bfloat16
AF = mybir.ActivationFunctionType
ALU = mybir.AluOpType
AX = mybir.AxisListType

ENGS = ("tensor", "vector", "scalar", "gpsimd", "sync")
NCORES = 8
D = 2048
NEG = -30000.0


class Buf:
    __slots__ = ("name", "writer", "readers", "dsem")

    def __init__(self, name):
        self.name = name
        self.writer = None
        self.readers = []
        self.dsem = None


class Prog:
    def __init__(self, nc, stack):
        self.nc = nc
        self.stack = stack
        self.ops = {e: [] for e in ENGS}
        self.sems = {}
        self.count = {}
        self.known = {e: {} for e in ENGS}
        for e in ENGS:
            self._mksem("E_" + e)
        self.n_dsem = 0
        self.dsem_pool = []

    def _mksem(self, key):
        self.sems[key] = self.stack.enter_context(self.nc.semaphore(key))
        self.count[key] = 0

    def dsem_of(self, buf):
        if buf.dsem is None:
            if self.dsem_pool:
                key = self.dsem_pool.pop()
            else:
                key = "D%d" % self.n_dsem
                self.n_dsem += 1
                self._mksem(key)
            buf.dsem = key
        return buf.dsem

    def _deps(self, eng, reads, writes):
        need = {}

        def add(ev):
            if ev is None:
                return
            k, v = ev
            if eng == "tensor" and k == "E_tensor":
                return
            if need.get(k, 0) < v:
                need[k] = v
        for b in reads:
            add(b.writer)
        for b in writes:
            add(b.writer)
            for r in b.readers:
                add(r)
        waits = []
        kn = self.known[eng]
        for k, v in need.items():
            if kn.get(k, 0) < v:
                kn[k] = v
                waits.append((k, v))
        return waits

    def _commit(self, ev, reads, writes):
        for b in reads:
            b.readers.append(ev)
            if len(b.readers) > 16:
                mx = {}
                for k, v in b.readers:
                    if mx.get(k, 0) < v:
                        mx[k] = v
                b.readers = list(mx.items())
        for b in writes:
            b.writer = ev
            b.readers = []

    disabled = False

    def op(self, eng, fn, reads=(), writes=()):
        if self.disabled:
            return None
        waits = self._deps(eng, reads, writes)
        key = "E_" + eng
        self.count[key] += 1
        ev = (key, self.count[key])
        self.ops[eng].append((waits, fn, (key, 1)))
        self._commit(ev, reads, writes)
        return ev

    def dma(self, eng, fn, sbuf, reads=(), writes=(), inc=16):
        if self.disabled:
            return None
        waits = self._deps(eng, reads, writes)
        key = self.dsem_of(sbuf)
        self.count[key] += inc
        ev = (key, self.count[key])
        self.ops[eng].append((waits, fn, (key, inc)))
        self._commit(ev, reads, writes)
        return ev

    def emit(self):
        nc = self.nc
        sems = self.sems
        ops = self.ops
        fw = {k: v for k, v in self.count.items() if v > 0}
        with nc.Block() as block:
            def runner(ename):
                def run(eng):
                    for waits, fn, inc in ops[ename]:
                        for k, v in waits:
                            eng.wait_ge(sems[k], v)
                        ins = fn(eng)
                        ins.then_inc(sems[inc[0]], inc[1])
                    for k, v in fw.items():
                        eng.wait_ge(sems[k], v)
                return run
            block.tensor(runner("tensor"))
            block.vector(runner("vector"))
            block.scalar(runner("scalar"))
            block.gpsimd(runner("gpsimd"))
            block.sync(runner("sync"))
        for e in ENGS:
            self.ops[e] = []
            for k, v in fw.items():
                self.known[e][k] = v


class T:
    __slots__ = ("t", "b")

    def __init__(self, t, name):
        self.t = t
        self.b = Buf(name)


_UID = [0]


class Cx:
    def __init__(self, nc, P):
        self.nc = nc
        self.P = P
        self.st = ExitStack()
        self.n = 0
        self.tiles = []

    def sb(self, name, shape, dt):
        _UID[0] += 1
        nm = "%s_%d" % (name, _UID[0])
        t = T(self.st.enter_context(self.nc.sbuf_tensor(nm, shape, dt)), nm)
        self.tiles.append(t)
        return t

    def ps(self, name, shape, dt=F32):
        _UID[0] += 1
        nm = "%s_%d" % (name, _UID[0])
        return T(self.st.enter_context(self.nc.psum_tensor(nm, shape, dt)), nm)

    def close(self):
        self.P.emit()
        self.st.close()
        for t in self.tiles:
            if t.b.dsem is not None:
                self.P.dsem_pool.append(t.b.dsem)
                t.b.dsem = None
        self.tiles = []


class DT:
    def __init__(self, nc, name, shape, dt, kind):
        self.h = nc.dram_tensor(name, list(shape), dt, kind=kind)
        self.ap = self.h.ap()
        self.b = Buf(name)
        self.shape = list(shape)

    def view(self, offset, ap):
        return bass.AP(tensor=self.h, offset=offset, ap=ap)


def dq(k):
    return ("sync", "scalar")[k % 2]

def load_gain_fm(cx, vec_ap_1d_handle, row):
    P = cx.P
    g = cx.sb("gfm", [128, 16], F32)
    src = vec_ap_1d_handle.view(row * D, [[1, 128], [128, 16]])
    P.dma("sync", lambda e: e.dma_start(out=g.t[:], in_=src, allow_slow_non_contiguous=True), g.b,
          reads=[vec_ap_1d_handle.b], writes=[g.b])
    return g


def load_gain_bc(cx, dth, row, n=D):
    P = cx.P
    g = cx.sb("gbc", [128, n], F32)
    src = dth.view(row * n, [[0, 128], [1, n]])
    P.dma("sync", lambda e: e.dma_start(out=g.t[:], in_=src), g.b, reads=[dth.b], writes=[g.b])
    return g


class NormT:
    def __init__(self, cx, identb):
        self.cx = cx
        self.identb = identb
        self.junk = cx.sb("nt_junk", [128, D], F32)
        self.ssq = [cx.sb("nt_ssq", [128, 1], F32) for _ in range(2)]
        self.rstd = [cx.sb("nt_rstd", [128, 1], F32) for _ in range(2)]
        self.xn = [cx.sb("nt_xn", [128, D], BF16) for _ in range(2)]
        self.pT = [cx.ps("nt_pT", [128, 4, 128], BF16) for _ in range(2)]
        self.k = 0
        self.kp = 0

    def run(self, xt, gfm, hT, tok0, eps=1e-6, ncols=D):
        P = self.cx.P
        k = self.k % 2
        self.k += 1
        junk, ssq, rstd, xn = self.junk, self.ssq[k], self.rstd[k], self.xn[k]
        P.op("scalar", lambda e: e.activation(out=junk.t[:, 0:ncols], in_=xt.t[:, 0:ncols], func=AF.Square, accum_out=ssq.t[:]),
             reads=[xt.b], writes=[junk.b, ssq.b])
        P.op("scalar", lambda e: e.activation(out=rstd.t[:], in_=ssq.t[:], func=AF.Sqrt, scale=1.0 / ncols, bias=eps),
             reads=[ssq.b], writes=[rstd.b])
        P.op("vector", lambda e: e.reciprocal(out=rstd.t[:], in_=rstd.t[:]), reads=[rstd.b], writes=[rstd.b])
        P.op("vector", lambda e: e.tensor_scalar(out=xn.t[:, 0:ncols], in0=xt.t[:, 0:ncols], scalar1=rstd.t[:, 0:1], scalar2=None, op0=ALU.mult),
             reads=[xt.b, rstd.b], writes=[xn.b])
        for g4 in range(ncols // 512):
            pT = self.pT[self.kp % 2]
            self.kp += 1
            for j in range(4):
                c = g4 * 4 + j
                P.op("tensor", lambda e, c=c, j=j, pT=pT: e.transpose(out=pT.t[:, j, :], in_=xn.t[:, c * 128:(c + 1) * 128], identity=self.identb.t[:]),
                     reads=[xn.b, self.identb.b], writes=[pT.b])
            P.op("vector", lambda e, g4=g4, pT=pT: e.tensor_tensor(
                out=hT.t[:, g4 * 4:(g4 + 1) * 4, tok0:tok0 + 128], in0=pT.t[:],
                in1=gfm.t[:, g4 * 4:(g4 + 1) * 4].unsqueeze(2).to_broadcast([128, 4, 128]), op=ALU.mult),
                reads=[pT.b, gfm.b], writes=[hT.b])
        return rstd


def make_ident(cx, ident_d):
    P = cx.P
    idf = cx.sb("idf", [128, 128], F32)
    idb = cx.sb("idb", [128, 128], BF16)
    P.dma("sync", lambda e: e.dma_start(out=idf.t[:], in_=ident_d.ap[:, :]), idf.b, reads=[ident_d.b], writes=[idf.b])
    P.op("vector", lambda e: e.tensor_copy(out=idb.t[:], in_=idf.t[:]), reads=[idf.b], writes=[idb.b])
    return idf, idb


ZQ, ZKV, ZGL, ZU, ZML = 0, 1024, 2560, 2608, 3632
IN_WIDTH = 7728


def phase_A(nc, P, NT, l, dr):
    NTOK = NT * 128
    NB = NT // 4
    cx = Cx(nc, P)
    idf, idb = make_ident(cx, dr["ident"])
    gfm = load_gain_fm(cx, dr["ln_mix_pre"], l)
    nrm = NormT(cx, idb)
    hT = [cx.sb("hT", [128, 16, 512], BF16) for _ in range(NB)]
    xin = [cx.sb("xin", [128, D], F32) for _ in range(2)]
    x = dr["x"]
    for t in range(NT):
        xt = xin[t % 2]
        P.dma("sync", lambda e, t=t, xt=xt: e.dma_start(out=xt.t[:], in_=x.ap[t * 128:(t + 1) * 128, :]), xt.b,
              reads=[x.b], writes=[xt.b])
        nrm.run(xt, gfm, hT[t // 4], (t % 4) * 128)
    w_in = dr["w_in"]
    wbuf = [cx.sb("wA", [128, 16, 512], BF16) for _ in range(2)]
    psum = [cx.ps("pA", [128, 512], F32) for _ in range(4)]
    stg = [cx.sb("stgA", [128, 512], F32) for _ in range(4)]
    stgb = [cx.sb("stgAb", [128, 512], BF16) for _ in range(4)]
    cnt = {"w": 0, "p": 0, "s": 0}

    def load_w(col0, ncols):
        wb = wbuf[cnt["w"] % 2]
        cnt["w"] += 1
        src = w_in.view(l * D * IN_WIDTH + col0, [[IN_WIDTH, 128], [128 * IN_WIDTH, 16], [1, ncols]])
        P.dma("gpsimd", lambda e: e.dma_start(out=wb.t[:, :, 0:ncols], in_=src), wb.b, reads=[w_in.b], writes=[wb.b])
        return wb

    def fm_group(col0, ncols, sub, evac):
        wb = load_w(col0, ncols)
        for j in range(ncols // sub):
            for blk in range(NB):
                ps = psum[cnt["p"] % 4]
                cnt["p"] += 1
                for kc in range(16):
                    P.op("tensor", lambda e, kc=kc, j=j, blk=blk, ps=ps: e.matmul(
                        ps.t[0:sub, :], lhsT=wb.t[:, kc, j * sub:(j + 1) * sub], rhs=hT[blk].t[:, kc, :],
                        start=(kc == 0), stop=(kc == 15)), reads=[wb.b, hT[blk].b], writes=[ps.b])
                evac(ps, j, blk)

    def tm_group(col0, ncols, evac):
        wb = load_w(col0, ncols)
        for t in range(NT):
            ps = psum[cnt["p"] % 4]
            cnt["p"] += 1
            for kc in range(16):
                P.op("tensor", lambda e, kc=kc, t=t, ps=ps: e.matmul(
                    ps.t[:, 0:ncols], lhsT=hT[t // 4].t[:, kc, (t % 4) * 128:(t % 4 + 1) * 128], rhs=wb.t[:, kc, 0:ncols],
                    start=(kc == 0), stop=(kc == 15)), reads=[wb.b, hT[t // 4].b], writes=[ps.b])
            evac(ps, t)

    def nxt(lst):
        s = lst[cnt["s"] % 4]
        cnt["s"] += 1
        return s

    qT = dr["qT"]
    for c0 in (0, 512):
        def ev_q(ps, j, blk, c0=c0):
            s = nxt(stgb)
            h = c0 // 64 + j
            P.op("scalar", lambda e: e.mul(out=s.t[0:64, :], in_=ps.t[0:64, :], mul=0.125), reads=[ps.b], writes=[s.b])
            P.dma("sync", lambda e: e.dma_start(out=qT.ap[h, :, blk * 512:(blk + 1) * 512], in_=s.t[0:64, :]), s.b,
                  reads=[s.b], writes=[qT.b])
        fm_group(ZQ + c0, 512, 64, ev_q)
    kT, vcT = dr["kT"], dr["vcT"]
    for kvi, dst, di in ((0, kT, 0), (1, vcT, None), (2, kT, 1), (4, kT, 2)):
        def ev_k(ps, j, blk, dst=dst, di=di):
            s = nxt(stgb)
            P.op("vector", lambda e: e.tensor_copy(out=s.t[0:64, :], in_=ps.t[0:64, :]), reads=[ps.b], writes=[s.b])
            o = dst.ap[di, j, :, blk * 512:(blk + 1) * 512] if di is not None else dst.ap[j, :, blk * 512:(blk + 1) * 512]
            P.dma("sync", lambda e: e.dma_start(out=o, in_=s.t[0:64, :]), s.b, reads=[s.b], writes=[dst.b])
        fm_group(ZKV + kvi * 256, 256, 64, ev_k)
    v = dr["v"]
    for kvi, di in ((3, 0), (5, 1)):
        def ev_v(ps, t, di=di):
            s = nxt(stgb)
            P.op("vector", lambda e: e.tensor_copy(out=s.t[:, 0:256], in_=ps.t[:, 0:256]), reads=[ps.b], writes=[s.b])
            P.dma("sync", lambda e: e.dma_start(out=v.ap[di, t * 128:(t + 1) * 128, :], in_=s.t[:, 0:256]), s.b,
                  reads=[s.b], writes=[v.b])
        tm_group(ZKV + kvi * 256, 256, ev_v)
    gl = dr["gl"]

    def ev_gl(ps, t):
        s = nxt(stg)
        P.op("vector", lambda e: e.tensor_copy(out=s.t[:, 0:48], in_=ps.t[:, 0:48]), reads=[ps.b], writes=[s.b])
        P.dma("sync", lambda e: e.dma_start(out=gl.ap[t * 128:(t + 1) * 128, :], in_=s.t[:, 0:48]), s.b, reads=[s.b], writes=[gl.b])
    tm_group(ZGL, 48, ev_gl)
    u = dr["u"]
    for c0 in (0, 512):
        def ev_u(ps, t, c0=c0):
            s = nxt(stg)
            P.op("scalar", lambda e: e.copy(out=s.t[:], in_=ps.t[:]), reads=[ps.b], writes=[s.b])
            P.dma("sync", lambda e: e.dma_start(out=u.ap[t * 128:(t + 1) * 128, c0:c0 + 512], in_=s.t[:]), s.b, reads=[s.b], writes=[u.b])
        tm_group(ZU + c0, 512, ev_u)
    gT = dr["gT"]
    for c0 in range(0, 4096, 512):
        def ev_g(ps, j, blk, c0=c0):
            s = nxt(stg)
            P.op("scalar", lambda e: e.activation(out=s.t[:], in_=ps.t[:], func=AF.Sigmoid), reads=[ps.b], writes=[s.b])
            r0 = c0 + j * 128
            P.dma("sync", lambda e: e.dma_start(out=gT.ap[r0:r0 + 128, blk * 512:(blk + 1) * 512], in_=s.t[:]), s.b, reads=[s.b], writes=[gT.b])
        fm_group(ZML + c0, 512, 128, ev_g)
    cx.close()

import math


def rel_bucket_np(dist):
    n = np.maximum(dist, 0)
    nf = np.maximum(n, 1).astype(np.float32)
    large = 16 + (np.log(nf / np.float32(16)) / np.float32(math.log(2048 / 16)) * np.float32(16)).astype(np.int32)
    return np.where(n < 16, n, np.minimum(large, 31))


LEN_S, M_S = 3200, 3072
LEN_W, M_W = 1664, 1536
LEN_C = 5248


def onehot_table(dist, win=None):
    L = dist.shape[0]
    oh = np.zeros((33, L), np.float32)
    masked = dist < 0
    if win is not None:
        masked = masked | (dist >= win)
    b = rel_bucket_np(dist)
    ok = ~masked
    oh[b[ok], np.nonzero(ok)[0]] += 1.0
    oh[31, ok] -= 1.0
    oh[32, masked] = NEG
    return oh


def core_tables(c, NT):
    SEQ = NT * 8 * 128
    n_sel = SEQ // 64
    NNT = SEQ // 2048
    n_cmp = SEQ // 16 - 1
    t = {}
    xs = np.arange(LEN_S)
    t["oh_sel"] = onehot_table(xs + 128 * (c - 7) - 127)
    xw = np.arange(LEN_W)
    t["oh_win"] = onehot_table(xw + 128 * (c - 7) - 127, win=512)
    xc = np.arange(LEN_C)
    t["oh_cmp"] = onehot_table(xc + 128 * c - 31 - 2032)
    q = np.arange(128)[:, None]
    m = np.arange(2 * n_sel)[None, :]
    rel = m - n_sel - 2 * c
    cur = (q >= 64).astype(np.int64)
    keep = np.ones((128, 2 * n_sel), np.float32)
    add = np.zeros((128, 2 * n_sel), np.float32)
    fut = rel > cur
    keep[fut] = 0.0
    add[fut] = -1.0
    is_cur = rel == cur
    keep[np.broadcast_to(is_cur, keep.shape)] = 0.0
    add[np.broadcast_to(is_cur, keep.shape)] = 10001.0
    is_prev = rel == cur - 1
    keep[np.broadcast_to(is_prev, keep.shape)] = 0.0
    add[np.broadcast_to(is_prev, keep.shape)] = 10002.0
    t["keepS"] = keep
    t["addS"] = add
    return t


def shared_tables(NT):
    SEQ = NT * 8 * 128
    n_sel = SEQ // 64
    NNT = SEQ // 2048
    n_cmp = SEQ // 16 - 1
    t = {}
    t["ident"] = np.eye(128, dtype=np.float32)
    t["jrev"] = np.eye(128, dtype=np.float32)[::-1].copy()
    jj = np.arange(128)[:, None]
    mm = np.arange(8192)[None, :]
    t["wstrip"] = (mm // 64 == jj).astype(np.float32)
    sa = np.zeros((48, 48 * 64), np.float32)
    for s in range(48):
        sa[s, s * 64:(s + 1) * 64] = 1.0
    t["selall"] = sa
    n = np.arange(NNT * 128)[:, None]
    j = np.arange(n_sel)[None, :]
    lo = np.maximum(n * 16, j * 64)
    hi = np.minimum(n * 16 + 32, (j + 1) * 64)
    ov = np.maximum(hi - lo, 0).astype(np.float32) / 32
    ov[n_cmp:, :] = 0.0
    t["ovl"] = ov.reshape(NNT, 128, n_sel).transpose(1, 0, 2).copy()
    return t


def phase_B(nc, P, NT, l, dr):
    NTOK = NT * 128
    SEQ = NTOK * 8
    n_sel = SEQ // 64
    NSELP = ((n_sel + 127) // 128) * 128
    NCH = NSELP // 128
    NNT = SEQ // 2048
    n_cmp = SEQ // 16 - 1
    NCP = NNT * 128
    cx = Cx(nc, P)
    idf, idb = make_ident(cx, dr["ident"])

    def load_const(name, shape, dt, src_dt):
        tl = cx.sb(name, shape, dt)
        eng = "sync" if dt == F32 else "gpsimd"
        P.dma(eng, lambda e: e.dma_start(out=tl.t[:], in_=src_dt.ap), tl.b, reads=[src_dt.b], writes=[tl.b])
        return tl
    jb = load_const("jb", [128, 128], BF16, dr["jrev"])
    wst = load_const("wst", [128, 8192], BF16, dr["wstrip"])
    selall = load_const("selall", [48, 48 * 64], F32, dr["selall"])
    ovl = load_const("ovl", [128, NNT, n_sel], BF16, dr["ovl"])
    keepS = load_const("keepS", [128, 2 * n_sel], F32, dr["keepS"])
    addS = load_const("addS", [128, 2 * n_sel], F32, dr["addS"])
    ones65 = cx.sb("ones65", [65, 64], F32)
    P.op("vector", lambda e: e.memset(ones65.t[:], 1.0), writes=[ones65.b])
    kcTa = cx.sb("kcTa", [65, 4, NCP], BF16)
    vca = cx.sb("vca", [128, 4, NNT, 65], BF16)
    P.op("vector", lambda e: e.memset(kcTa.t[:], 0.0), writes=[kcTa.b])
    P.op("vector", lambda e: e.memset(kcTa.t[64:65, :, :], 1.0), writes=[kcTa.b])
    P.op("vector", lambda e: e.memset(vca.t[:], 0.0), writes=[vca.b])
    P.op("vector", lambda e: e.memset(vca.t[:, :, :, 64:65], 1.0), writes=[vca.b])
    sgT = cx.sb("sgT", [48, NTOK], F32)
    b31 = cx.sb("b31", [65, 16], F32)
    rb = dr["rel_bias"]
    P.dma("sync", lambda e: e.dma_start(out=b31.t[64:65, :], in_=rb.view(31, [[0, 1], [32, 16]]), allow_slow_non_contiguous=True),
          b31.b, reads=[rb.b], writes=[b31.b])

    c0 = Cx(nc, P)
    tab33 = c0.sb("tab33", [33, 16], F32)
    P.op("vector", lambda e: e.memset(tab33.t[32:33, :], 1.0), writes=[tab33.b])
    P.dma("sync", lambda e: e.dma_start(out=tab33.t[0:32, :], in_=rb.view(0, [[1, 32], [32, 16]]), allow_slow_non_contiguous=True),
          tab33.b, reads=[rb.b], writes=[tab33.b])
    psg = [c0.ps("psg", [16, 512], F32) for _ in range(2)]
    kk = 0
    for name, LEN in (("sel", LEN_S), ("win", LEN_W), ("cmp", LEN_C)):
        oh = c0.sb("oh_" + name, [33, LEN], F32)
        src = dr["oh_" + name]
        P.dma("sync", lambda e, oh=oh, src=src: e.dma_start(out=oh.t[:], in_=src.ap), oh.b, reads=[src.b], writes=[oh.b])
        gs = c0.sb("gs_" + name, [16, LEN], BF16)
        for x0 in range(0, LEN, 512):
            n = min(512, LEN - x0)
            ps = psg[kk % 2]
            kk += 1
            P.op("tensor", lambda e, ps=ps, oh=oh, x0=x0, n=n: e.matmul(ps.t[:, 0:n], lhsT=tab33.t[:], rhs=oh.t[:, x0:x0 + n], start=True, stop=True),
                 reads=[tab33.b, oh.b], writes=[ps.b])
            P.op("vector", lambda e, ps=ps, gs=gs, x0=x0, n=n: e.tensor_copy(out=gs.t[:, x0:x0 + n], in_=ps.t[:, 0:n]), reads=[ps.b], writes=[gs.b])
        gd = dr["gd_" + name]
        P.dma("sync", lambda e, gs=gs, gd=gd: e.dma_start(out=gd.ap, in_=gs.t[:]), gs.b, reads=[gs.b], writes=[gd.b])
    glt = [c0.sb("glt", [128, 48], F32) for _ in range(2)]
    pst = [c0.ps("pst", [48, 128], F32) for _ in range(2)]
    gl = dr["gl"]
    for t in range(NT):
        g_ = glt[t % 2]
        ps = pst[t % 2]
        P.dma("sync", lambda e, t=t, g_=g_: e.dma_start(out=g_.t[:], in_=gl.ap[t * 128:(t + 1) * 128, :]), g_.b, reads=[gl.b], writes=[g_.b])
        P.op("scalar", lambda e, g_=g_: e.activation(out=g_.t[:], in_=g_.t[:], func=AF.Sigmoid), reads=[g_.b], writes=[g_.b])
        P.op("tensor", lambda e, g_=g_, ps=ps: e.transpose(out=ps.t[:], in_=g_.t[:], identity=idf.t[:]), reads=[g_.b, idf.b], writes=[ps.b])
        P.op("vector", lambda e, t=t, ps=ps: e.tensor_copy(out=sgT.t[:, t * 128:(t + 1) * 128], in_=ps.t[:]), reads=[ps.b], writes=[sgT.b])
    c0.close()

    c1 = Cx(nc, P)
    kTg, vcTg = dr["kTg"], dr["vcTg"]
    raw2 = [c1.sb("raw2", [128, SEQ + 1040], BF16) for _ in range(2)]
    w1b = [c1.sb("w1b", [128, 16, 128], BF16) for _ in range(2)]
    w2b = [c1.sb("w2b", [128, 64], BF16) for _ in range(2)]
    pe2 = [c1.sb("pe2", [128, 16], BF16) for _ in range(2)]
    biasv = [c1.sb("biasv", [128, 1], F32) for _ in range(2)]
    psb = c1.ps("psb", [128, 1], F32)
    psh = [c1.ps("psh", [128, 512], F32) for _ in range(2)]
    pso = [c1.ps("pso", [128, 512], F32) for _ in range(2)]
    xh = c1.sb("xh", [128, 512], F32)
    tq = c1.sb("tq", [128, 512], F32)
    sg = c1.sb("sg", [128, 512], F32)
    gel = [c1.sb("gel", [128, 512], BF16) for _ in range(2)]
    w1d, w2d, ped = dr["cmp_w1"], dr["cmp_w2"], dr["cmp_pe"]
    for r2 in raw2:
        P.op("vector", lambda e, r2=r2: e.memset(r2.t[:], 0.0), writes=[r2.b])
    kh = 0
    for kvi in range(2):
        P.dma("gpsimd", lambda e, kvi=kvi: e.dma_start(out=w1b[kvi].t[:], in_=w1d.view((l * 2 + kvi) * 2048 * 128, [[128, 128], [128 * 128, 16], [1, 128]])),
              w1b[kvi].b, reads=[w1d.b], writes=[w1b[kvi].b])
        P.dma("gpsimd", lambda e, kvi=kvi: e.dma_start(out=w2b[kvi].t[:], in_=w2d.ap[l, kvi, :, :]), w2b[kvi].b, reads=[w2d.b], writes=[w2b[kvi].b])
        P.dma("gpsimd", lambda e, kvi=kvi: e.dma_start(out=pe2[kvi].t[:], in_=ped.view((l * 2 + kvi) * 2048, [[1, 128], [128, 16]]), allow_slow_non_contiguous=True),
              pe2[kvi].b, reads=[ped.b], writes=[pe2[kvi].b])
        for c in range(16):
            P.op("tensor", lambda e, kvi=kvi, c=c: e.matmul(psb.t[:], lhsT=w1b[kvi].t[:, c, :], rhs=pe2[kvi].t[:, c:c + 1], start=(c == 0), stop=(c == 15)),
                 reads=[w1b[kvi].b, pe2[kvi].b], writes=[psb.b])
        P.op("vector", lambda e, kvi=kvi: e.tensor_copy(out=biasv[kvi].t[:], in_=psb.t[:]), reads=[psb.b], writes=[biasv[kvi].b])
        for g in range(4):
            r2 = raw2[(kvi * 4 + g) % 2]
            for r in range(8):
                if kvi == 0:
                    src = kTg.ap[r, 0, g, :, :]
                else:
                    src = vcTg.ap[r, g, :, :]
                src = src.rearrange("d (i p) -> d i p", p=128)
                for half, off in ((0, 1), (1, 0)):
                    base = off + r * 128
                    P.dma("sync", lambda e, r2=r2, src=src, half=half, base=base: e.dma_start(
                        out=r2.t[half * 64:(half + 1) * 64, base:base + NT * 1024].rearrange("d (i x) -> d i x", x=1024)[:, :, 0:128], in_=src),
                        r2.b, reads=[kTg.b if kvi == 0 else vcTg.b], writes=[r2.b])
            for n0 in range(0, n_cmp, 512):
                nn = min(512, n_cmp - n0)
                ph = psh[kh % 2]
                gl_ = gel[kh % 2]
                po = pso[kh % 2]
                kh += 1
                for c in range(16):
                    s0 = 1 + 16 * n0 + 2 * c
                    P.op("tensor", lambda e, c=c, ph=ph, r2=r2, s0=s0, nn=nn, kvi=kvi: e.matmul(
                        ph.t[:, 0:nn], lhsT=w1b[kvi].t[:, c, :], rhs=r2.t[:, s0:s0 + 16 * (nn - 1) + 1:16], start=(c == 0), stop=(c == 15)),
                        reads=[w1b[kvi].b, r2.b], writes=[ph.b])
                P.op("scalar", lambda e, ph=ph, nn=nn, kvi=kvi: e.activation(out=xh.t[:, 0:nn], in_=ph.t[:, 0:nn], func=AF.Identity, bias=biasv[kvi].t[:, 0:1]),
                     reads=[ph.b, biasv[kvi].b], writes=[xh.b])
                P.op("vector", lambda e, nn=nn: e.tensor_tensor(out=tq.t[:, 0:nn], in0=xh.t[:, 0:nn], in1=xh.t[:, 0:nn], op=ALU.mult), reads=[xh.b], writes=[tq.b])
                P.op("vector", lambda e, nn=nn: e.tensor_scalar(out=tq.t[:, 0:nn], in0=tq.t[:, 0:nn], scalar1=0.044715, scalar2=1.0, op0=ALU.mult, op1=ALU.add),
                     reads=[tq.b], writes=[tq.b])
                P.op("vector", lambda e, nn=nn: e.tensor_tensor(out=tq.t[:, 0:nn], in0=tq.t[:, 0:nn], in1=xh.t[:, 0:nn], op=ALU.mult), reads=[tq.b, xh.b], writes=[tq.b])
                P.op("scalar", lambda e, nn=nn: e.activation(out=sg.t[:, 0:nn], in_=tq.t[:, 0:nn], func=AF.Sigmoid, scale=1.5957691216057308),
                     reads=[tq.b], writes=[sg.b])
                P.op("vector", lambda e, nn=nn, gl_=gl_: e.tensor_tensor(out=gl_.t[:, 0:nn], in0=xh.t[:, 0:nn], in1=sg.t[:, 0:nn], op=ALU.mult),
                     reads=[xh.b, sg.b], writes=[gl_.b])
                if kvi == 0:
                    P.op("tensor", lambda e, po=po, gl_=gl_, nn=nn: e.matmul(po.t[0:64, 0:nn], lhsT=w2b[0].t[:], rhs=gl_.t[:, 0:nn], start=True, stop=True),
                         reads=[w2b[0].b, gl_.b], writes=[po.b])
                    P.op("scalar", lambda e, po=po, g=g, n0=n0, nn=nn: e.copy(out=kcTa.t[0:64, g, n0:n0 + nn], in_=po.t[0:64, 0:nn]), reads=[po.b], writes=[kcTa.b])
                else:
                    for s in range(0, nn, 128):
                        ns = min(128, nn - s)
                        P.op("tensor", lambda e, po=po, gl_=gl_, s=s, ns=ns: e.matmul(po.t[0:ns, s // 128 * 64:s // 128 * 64 + 64], lhsT=gl_.t[:, s:s + ns], rhs=w2b[1].t[:],
                                                                               start=True, stop=True), reads=[w2b[1].b, gl_.b], writes=[po.b])
                    for s in range(0, nn, 128):
                        ns = min(128, nn - s)
                        nt = (n0 + s) // 128
                        P.op("scalar", lambda e, po=po, g=g, s=s, ns=ns, nt=nt: e.copy(out=vca.t[0:ns, g, nt, 0:64], in_=po.t[0:ns, s // 128 * 64:s // 128 * 64 + 64]),
                             reads=[po.b], writes=[vca.b])
    c1.close()
    return cx, dict(idf=idf, idb=idb, jb=jb, wst=wst, selall=selall, ovl=ovl, keepS=keepS, addS=addS, ones65=ones65,
                    kcTa=kcTa, vca=vca, sgT=sgT, b31=b31)


def phase_B2(nc, P, NT, l, dr, cx, env):
    NTOK = NT * 128
    SEQ = NTOK * 8
    n_sel = SEQ // 64
    NSELP = ((n_sel + 127) // 128) * 128
    NCH = NSELP // 128
    NNT = SEQ // 2048
    idf, idb, jb, wst, selall, ovl = env["idf"], env["idb"], env["jb"], env["wst"], env["selall"], env["ovl"]
    keepS, addS, ones65, kcTa, vca, sgT, b31 = env["keepS"], env["addS"], env["ones65"], env["kcTa"], env["vca"], env["sgT"], env["b31"]
    kTg, vg, qT, aT = dr["kTg"], dr["vg"], dr["qT"], dr["aT"]
    qta = cx.sb("qta", [65, 4, NTOK], BF16)
    kTs = cx.sb("kTs", [65, 8, NTOK], BF16)
    vs = cx.sb("vs", [128, 8, NT, 65], BF16)
    Fs = cx.sb("Fs", [128, 4, M_S], BF16)
    Fw = cx.sb("Fw", [128, 4, M_W], BF16)
    Fc = cx.sb("Fc", [128, 4, 4, 128], BF16)
    P.op("vector", lambda e: e.memset(kTs.t[64:65, :, :], 1.0), writes=[kTs.b])
    P.op("vector", lambda e: e.memset(vs.t[:, :, :, 64:65], 1.0), writes=[vs.b])
    kTw = [cx.sb("kTw", [65, 12, 128], BF16) for _ in range(2)]
    vw = [cx.sb("vw", [128, 12, 65], BF16) for _ in range(2)]
    for b_ in kTw:
        P.op("vector", lambda e, b_=b_: e.memset(b_.t[64:65, :, :], 1.0), writes=[b_.b])
    for b_ in vw:
        P.op("vector", lambda e, b_=b_: e.memset(b_.t[:, :, 64:65], 1.0), writes=[b_.b])
    psR = [cx.ps("psR", [128, 512], F32) for _ in range(3)]
    psOs = cx.ps("psOs", [65, 512], F32)
    psOw = cx.ps("psOw", [65, 512], F32)
    psOc = cx.ps("psOc", [65, 512], F32)
    psU = [cx.ps("psU", [128, 2, 256], F32) for _ in range(2)]
    Eb = [cx.sb("Eb", [128, 512], BF16) for _ in range(4)]
    Osb = [cx.sb("Osb", [65, 512], F32) for _ in range(2)]
    rrec = cx.sb("rrec", [64, 512], F32)
    fac = cx.sb("fac", [64, 512], F32)
    acc = cx.sb("acc", [64, 512], F32)
    prod = cx.sb("prod", [64, 512], F32)
    aTs = [cx.sb("aTs", [64, 4, 128], BF16) for _ in range(2)]
    rs = cx.sb("rs", [128, 4], F32)
    imp = cx.sb("imp", [128, n_sel], F32)
    rep = cx.sb("rep", [128, n_sel], F32)
    m8 = cx.sb("m8", [128, 16], F32)
    selm = cx.sb("selm", [128, n_sel], F32)
    mb = cx.sb("mb", [128, NSELP], BF16)
    P.op("vector", lambda e: e.memset(mb.t[:], 0.0), writes=[mb.b])
    mbT = [cx.sb("mbT", [128, NCH, 4, 128], BF16) for _ in range(2)]
    st = {"r": 0, "e": 0, "w": 0, "u": 0}
    gd_sel, gd_win, gd_cmp = dr["gd_sel"], dr["gd_win"], dr["gd_cmp"]

    def nextR():
        p = psR[st["r"] % 3]
        st["r"] += 1
        return p

    def nextE():
        p = Eb[st["e"] % 4]
        st["e"] += 1
        return p

    for g in range(4):
        P.dma("sync", lambda e, g=g: e.dma_start(out=qta.t[0:64, :, :], in_=qT.ap[4 * g:4 * g + 4, :, :].rearrange("h d t -> d h t")),
              qta.b, reads=[qT.b], writes=[qta.b])
        P.op("vector", lambda e, g=g: e.tensor_copy(out=qta.t[64:65, :, :], in_=b31.t[64:65, 4 * g:4 * g + 4].unsqueeze(2).to_broadcast([1, 4, NTOK])),
             reads=[b31.b], writes=[qta.b])
        for r in range(8):
            P.dma("sync", lambda e, g=g, r=r: e.dma_start(out=kTs.t[0:64, r, :], in_=kTg.ap[r, 1, g, :, :]), kTs.b, reads=[kTg.b], writes=[kTs.b])
            P.dma("sync", lambda e, g=g, r=r: e.dma_start(out=vs.t[:, r, :, 0:64], in_=vg.ap[r, 0, :, g * 64:(g + 1) * 64].rearrange("(i p) d -> p i d", p=128)),
                  vs.b, reads=[vg.b], writes=[vs.b])
        P.dma("sync", lambda e, g=g: e.dma_start(out=Fs.t[:], in_=gd_sel.view(4 * g * LEN_S, [[1, 128], [LEN_S, 4], [1, M_S]])), Fs.b, reads=[gd_sel.b], writes=[Fs.b])
        P.dma("sync", lambda e, g=g: e.dma_start(out=Fw.t[:], in_=gd_win.view(4 * g * LEN_W, [[1, 128], [LEN_W, 4], [1, M_W]])), Fw.b, reads=[gd_win.b], writes=[Fw.b])
        for b_ in range(2):
            for e_ in range(2):
                P.dma("sync", lambda e, g=g, b_=b_, e_=e_: e.dma_start(out=Fc.t[:, b_ * 2 + e_, :, :],
                      in_=gd_cmp.view(4 * g * LEN_C + 1024 * b_ + 2048 * e_, [[16, 128], [LEN_C, 4], [1, 128]])), Fc.b, reads=[gd_cmp.b], writes=[Fc.b])
        for i in range(NT):
            u_ = st["u"]
            st["u"] += 1
            qsl = qta.t[:, :, i * 128:(i + 1) * 128]
            kw, vw_ = kTw[u_ % 2], vw[u_ % 2]
            wt = []
            for a, jr in ((0, range(8)), (1, range(4, 8))):
                ii = i - a
                if ii < 0:
                    continue
                for j in jr:
                    slot = j if a == 0 else 4 + j
                    wt.append((slot, a, j))
                j0, nj = (0, 8) if a == 0 else (4, 4)
                s0 = 0 if a == 0 else 8
                P.dma("sync", lambda e, g=g, ii=ii, j0=j0, nj=nj, s0=s0, kw=kw: e.dma_start(
                    out=kw.t[0:64, s0:s0 + nj, :], in_=kTg.ap[j0:j0 + nj, 2, g, :, ii * 128:(ii + 1) * 128].rearrange("r d p -> d r p")),
                    kw.b, reads=[kTg.b], writes=[kw.b])
                P.dma("sync", lambda e, g=g, ii=ii, j0=j0, nj=nj, s0=s0, vw_=vw_: e.dma_start(
                    out=vw_.t[:, s0:s0 + nj, 0:64], in_=vg.ap[j0:j0 + nj, 1, ii * 128:(ii + 1) * 128, g * 64:(g + 1) * 64].rearrange("r p d -> p r d")),
                    vw_.b, reads=[vg.b], writes=[vw_.b])
            nts = list(range(i // 2 + 1))
            ntl = nts[-1]
            for nt in nts:
                e_ = i // 2 - nt
                near = e_ <= 1
                S = nextR()
                P.op("tensor", lambda e, S=S, nt=nt, qsl=qsl, near=near, g=g: e.matmul(S.t[:], lhsT=kcTa.t[:, g, nt * 128:(nt + 1) * 128], rhs=qsl, start=True, stop=not near),
                     reads=[kcTa.b, qta.b], writes=[S.b])
                if near:
                    cmb = (i % 2) * 2 + e_
                    P.op("tensor", lambda e, S=S, cmb=cmb: e.matmul(S.t[:], lhsT=jb.t[:], rhs=Fc.t[:, cmb, :, :], start=False, stop=True),
                         reads=[jb.b, Fc.b], writes=[S.b])
                E = nextE()
                P.op("scalar", lambda e, S=S, E=E: e.activation(out=E.t[:], in_=S.t[:], func=AF.Exp), reads=[S.b], writes=[E.b])
                P.op("tensor", lambda e, E=E, nt=nt, g=g, ntl=ntl: e.matmul(psOc.t[:], lhsT=vca.t[:, g, nt, :], rhs=E.t[:], start=(nt == 0), stop=(nt == ntl)),
                     reads=[vca.b, E.b], writes=[psOc.b])
                for h in range(4):
                    P.op("tensor", lambda e, E=E, nt=nt, h=h, ntl=ntl: e.matmul(psU[h // 2].t[:, h % 2, 0:n_sel], lhsT=E.t[:, h * 128:(h + 1) * 128], rhs=ovl.t[:, nt, :],
                                                                    start=(nt == 0 and h % 2 == 0), stop=(nt == ntl), skip_group_check=True), reads=[ovl.b, E.b], writes=[psU[h // 2].b])
            def attend(tiles, psO, kfn, vfn, mask_fn, bias_fn, qsl=qsl, kw=kw, vw_=vw_):
                nt_ = len(tiles)
                for idx, tl in enumerate(tiles):
                    S = nextR()
                    mk = mask_fn(tl)
                    bs = bias_fn(tl)
                    ka = kfn(tl)
                    va = vfn(tl)
                    P.op("tensor", lambda e, S=S, ka=ka, mk=mk, bs=bs, qsl=qsl: e.matmul(S.t[:], lhsT=ka, rhs=qsl, start=True, stop=(mk is None and bs is None)),
                         reads=[kTs.b, kw.b, qta.b], writes=[S.b])
                    if mk is not None:
                        P.op("tensor", lambda e, S=S, mk=mk, bs=bs: e.matmul(S.t[:], lhsT=mk[0], rhs=mk[1], start=False, stop=(bs is None)),
                             reads=[wst.b, mk[2]], writes=[S.b])
                    if bs is not None:
                        P.op("tensor", lambda e, S=S, bs=bs: e.matmul(S.t[:], lhsT=jb.t[:], rhs=bs[0], start=False, stop=True),
                             reads=[jb.b, bs[1]], writes=[S.b])
                    E = nextE()
                    P.op("scalar", lambda e, S=S, E=E: e.activation(out=E.t[:], in_=S.t[:], func=AF.Exp), reads=[S.b], writes=[E.b])
                    P.op("tensor", lambda e, E=E, va=va, idx=idx, nt_=nt_, psO=psO: e.matmul(psO.t[:], lhsT=va, rhs=E.t[:], start=(idx == 0), stop=(idx == nt_ - 1)),
                         reads=[vs.b, vw_.b, E.b], writes=[psO.b])
            attend(wt, psOw,
                   lambda tl, kw=kw: kw.t[:, tl[0], :],
                   lambda tl, vw_=vw_: vw_.t[:, tl[0], :],
                   lambda tl: None,
                   lambda tl: (Fw.t[:, :, 128 * (8 * tl[1] + 7 - tl[2]):128 * (8 * tl[1] + 7 - tl[2]) + 128], Fw.b))
            for hh in range(2):
                P.op("vector", lambda e, hh=hh: e.tensor_reduce(out=rs.t[:, 2 * hh:2 * hh + 2], in_=psU[hh].t[:, :, 0:n_sel], axis=AX.X, op=ALU.add), reads=[psU[hh].b], writes=[rs.b])
            P.op("vector", lambda e: e.tensor_scalar(out=rs.t[:], in0=rs.t[:], scalar1=1e-30, scalar2=None, op0=ALU.max), reads=[rs.b], writes=[rs.b])
            P.op("vector", lambda e: e.reciprocal(out=rs.t[:], in_=rs.t[:]), reads=[rs.b], writes=[rs.b])
            P.op("vector", lambda e: e.tensor_scalar(out=imp.t[:], in0=psU[0].t[:, 0, 0:n_sel], scalar1=rs.t[:, 0:1], scalar2=None, op0=ALU.mult),
                 reads=[psU[0].b, rs.b], writes=[imp.b])
            for h in range(1, 4):
                P.op("vector", lambda e, h=h: e.scalar_tensor_tensor(out=imp.t[:], in0=psU[h // 2].t[:, h % 2, 0:n_sel], scalar=rs.t[:, h:h + 1], in1=imp.t[:], op0=ALU.mult, op1=ALU.add),
                     reads=[psU[h // 2].b, rs.b, imp.b], writes=[imp.b])
            so = n_sel - 16 * i
            P.op("vector", lambda e, so=so: e.tensor_tensor(out=imp.t[:], in0=imp.t[:], in1=keepS.t[:, so:so + n_sel], op=ALU.mult), reads=[imp.b, keepS.b], writes=[imp.b])
            P.op("vector", lambda e, so=so: e.tensor_tensor(out=imp.t[:], in0=imp.t[:], in1=addS.t[:, so:so + n_sel], op=ALU.add), reads=[imp.b, addS.b], writes=[imp.b])
            P.op("vector", lambda e: e.memset(imp.t[:, 0:1], 10000.0), reads=[imp.b], writes=[imp.b])
            P.op("vector", lambda e: e.max(out=m8.t[:, 0:8], in_=imp.t[:]), reads=[imp.b], writes=[m8.b])
            P.op("vector", lambda e: e.match_replace(out=rep.t[:], in_to_replace=m8.t[:, 0:8], in_values=imp.t[:], imm_value=-5.0), reads=[imp.b, m8.b], writes=[rep.b])
            P.op("vector", lambda e: e.max(out=m8.t[:, 8:16], in_=rep.t[:]), reads=[rep.b, m8.b], writes=[m8.b])
            P.op("vector", lambda e: e.tensor_scalar(out=selm.t[:], in0=imp.t[:], scalar1=m8.t[:, 15:16], scalar2=None, op0=ALU.is_ge), reads=[imp.b, m8.b], writes=[selm.b])
            P.op("vector", lambda e: e.scalar_tensor_tensor(out=selm.t[:], in0=imp.t[:], scalar=0.0, in1=selm.t[:], op0=ALU.is_ge, op1=ALU.mult),
                 reads=[imp.b, selm.b], writes=[selm.b])
            P.op("vector", lambda e: e.tensor_scalar(out=mb.t[:, 0:n_sel], in0=selm.t[:], scalar1=-NEG, scalar2=NEG, op0=ALU.mult, op1=ALU.add),
                 reads=[selm.b], writes=[mb.b])
            if "dbg" in dr and g == 0:
                dbg = dr["dbg"]
                P.dma("sync", lambda e, i=i: e.dma_start(out=dbg.ap[i, :, 0:n_sel], in_=imp.t[:]), imp.b, reads=[imp.b], writes=[dbg.b])
                P.dma("sync", lambda e, i=i: e.dma_start(out=dbg.ap[i, :, 256:256 + n_sel], in_=selm.t[:]), selm.b, reads=[selm.b], writes=[dbg.b])
                P.dma("sync", lambda e, i=i: e.dma_start(out=dbg.ap[i, :, 512:528], in_=m8.t[:]), m8.b, reads=[m8.b], writes=[dbg.b])
            mT = mbT[u_ % 2]
            for ch in range(NCH):
                pt = nextR()
                ptv = pt.t[:].bitcast(BF16)[:, 0:128]
                P.op("tensor", lambda e, ch=ch, ptv=ptv: e.transpose(out=ptv, in_=mb.t[:, ch * 128:(ch + 1) * 128], identity=idb.t[:]), reads=[mb.b, idb.b], writes=[pt.b])
                P.op("vector", lambda e, ch=ch, ptv=ptv, mT=mT: e.tensor_copy(out=mT.t[:, ch, :, :], in_=ptv.unsqueeze(1).to_broadcast([128, 4, 128])),
                     reads=[pt.b], writes=[mT.b])
            stl = [(ip, j) for ip in range(i + 1) for j in range(8)]
            attend(stl, psOs,
                   lambda tl: kTs.t[:, tl[1], tl[0] * 128:(tl[0] + 1) * 128],
                   lambda tl: vs.t[:, tl[1], tl[0], :],
                   lambda tl: (wst.t[:, 128 * ((8 * tl[0] + tl[1]) % 64):128 * ((8 * tl[0] + tl[1]) % 64) + 128], mT.t[:, (8 * tl[0] + tl[1]) // 64, :, :], mT.b),
                   lambda tl: ((Fs.t[:, :, 128 * (8 * (i - tl[0]) + 7 - tl[1]):128 * (8 * (i - tl[0]) + 7 - tl[1]) + 128], Fs.b) if i - tl[0] <= 2 else None))
            for bi, psO in enumerate((psOc, psOs, psOw)):
                osb = Osb[bi % 2]
                P.op("scalar", lambda e, osb=osb, psO=psO: e.copy(out=osb.t[:], in_=psO.t[:]), reads=[psO.b], writes=[osb.b])
                pd = nextR()
                P.op("tensor", lambda e, pd=pd, osb=osb: e.matmul(pd.t[0:64, :], lhsT=ones65.t[64:65, :], rhs=osb.t[64:65, :], start=True, stop=True),
                     reads=[ones65.b, osb.b], writes=[pd.b])
                P.op("vector", lambda e, pd=pd: e.tensor_scalar(out=rrec.t[:], in0=pd.t[0:64, :], scalar1=1e-30, scalar2=None, op0=ALU.max), reads=[pd.b], writes=[rrec.b])
                P.op("vector", lambda e: e.reciprocal(out=rrec.t[:], in_=rrec.t[:]), reads=[rrec.b], writes=[rrec.b])
                pg = nextR()
                for h in range(4):
                    srow = 3 * (4 * g + h) + bi
                    P.op("tensor", lambda e, pg=pg, h=h, srow=srow, i=i: e.matmul(pg.t[0:64, h * 128:(h + 1) * 128], lhsT=selall.t[:, srow * 64:(srow + 1) * 64],
                                                                           rhs=sgT.t[:, i * 128:(i + 1) * 128], start=True, stop=True), reads=[selall.b, sgT.b], writes=[pg.b])
                P.op("vector", lambda e, pg=pg: e.tensor_tensor(out=fac.t[:], in0=pg.t[0:64, :], in1=rrec.t[:], op=ALU.mult), reads=[pg.b, rrec.b], writes=[fac.b])
                if bi == 0:
                    P.op("vector", lambda e, osb=osb: e.tensor_tensor(out=acc.t[:], in0=osb.t[0:64, :], in1=fac.t[:], op=ALU.mult), reads=[osb.b, fac.b], writes=[acc.b])
                else:
                    P.op("gpsimd", lambda e, osb=osb: e.tensor_tensor(out=prod.t[:], in0=osb.t[0:64, :], in1=fac.t[:], op=ALU.mult), reads=[osb.b, fac.b], writes=[prod.b])
                    P.op("gpsimd", lambda e: e.tensor_tensor(out=acc.t[:], in0=acc.t[:], in1=prod.t[:], op=ALU.add), reads=[acc.b, prod.b], writes=[acc.b])
            ao = aTs[u_ % 2]
            P.op("vector", lambda e, ao=ao: e.tensor_copy(out=ao.t[:], in_=acc.t[:].rearrange("d (h q) -> d h q", h=4)), reads=[acc.b], writes=[ao.b])
            P.dma("sync", lambda e, ao=ao, g=g, i=i: e.dma_start(out=aT.ap[4 * g:4 * g + 4, :, i * 128:(i + 1) * 128].rearrange("h d t -> d h t"), in_=ao.t[:]),
                  ao.b, reads=[ao.b], writes=[aT.b])
    cx.close()

D_FF = 5632
C_STOP = 0
NFF = D_FF // 128


def pool_tables(c):
    wins = (2, 4, 8, 16)
    main = np.zeros((2, 128, 4, 128), np.float32)
    hA = np.zeros((128, 4, 128), np.float32)
    hB = np.zeros((16, 4, 128), np.float32)
    s = np.arange(128)[:, None]
    t = np.arange(128)[None, :]
    for gi, w in enumerate(wins):
        band = ((s <= t) & (s >= t - w + 1)).astype(np.float32)
        main[1, :, gi, :] = band / w - (s == t)
        if c == 0:
            cnt = np.minimum(t + 1, w).astype(np.float32)
            main[0, :, gi, :] = band / cnt - (s == t)
        else:
            main[0, :, gi, :] = main[1, :, gi, :]
        hs = np.arange(16)[:, None] - 16
        hb = ((hs >= t - w + 1)).astype(np.float32) / w
        if c >= 1:
            hA[(c - 1) * 16:(c - 1) * 16 + 16, gi, :] = hb
        else:
            hB[:, gi, :] = hb
    return {"poolM": main, "poolHA": hA, "poolHB": hB}


def post_norm_residual(cx, ysb, xt_ap, xb, gbc, scr, eps=1e-6):
    P = cx.P
    junk, ssq, rstd, tmp = scr
    P.op("scalar", lambda e: e.activation(out=junk.t[:], in_=ysb.t[:], func=AF.Square, accum_out=ssq.t[:]), reads=[ysb.b], writes=[junk.b, ssq.b])
    P.op("scalar", lambda e: e.activation(out=rstd.t[:], in_=ssq.t[:], func=AF.Sqrt, scale=1.0 / D, bias=eps), reads=[ssq.b], writes=[rstd.b])
    P.op("vector", lambda e: e.reciprocal(out=rstd.t[:], in_=rstd.t[:]), reads=[rstd.b], writes=[rstd.b])
    P.op("vector", lambda e: e.scalar_tensor_tensor(out=tmp.t[:], in0=ysb.t[:], scalar=rstd.t[:, 0:1], in1=gbc.t[:], op0=ALU.mult, op1=ALU.mult),
         reads=[ysb.b, rstd.b, gbc.b], writes=[tmp.b])
    P.op("gpsimd", lambda e: e.tensor_tensor(out=xt_ap, in0=xt_ap, in1=tmp.t[:], op=ALU.add), reads=[xb.b, tmp.b], writes=[xb.b])


def phase_C(nc, P, NT, l, dr):
    NTOK = NT * 128
    NB = NT // 4
    x, xo = dr["x"], dr["x_out"]
    aT, u, uhg, gT, mem = dr["aT"], dr["u"], dr["uhg"], dr["gT"], dr["mem"]
    cxo = Cx(nc, P)
    idf, idb = make_ident(cxo, dr["ident"])
    ones_b = cxo.sb("ones_b", [128, 128], BF16)
    P.op("vector", lambda e: e.memset(ones_b.t[:], 1.0), writes=[ones_b.b])
    KmT = cxo.sb("KmT", [128, 4, 256], BF16)
    Vm = cxo.sb("Vm", [128, 2, 512], BF16)
    xb = cxo.sb("xb", [128, 4, D], F32)
    tmp = cxo.sb("tmp", [128, D], F32)
    ssq = cxo.sb("ssq", [128, 1], F32)
    rstd = cxo.sb("rstd", [128, 1], F32)
    scr = (tmp, ssq, rstd, tmp)
    ysb = [cxo.sb("ysb", [128, D], F32) for _ in range(4)]

    def wload(cx_, name, src_dt, base, row_stride, kc, ncols, dst=None):
        tl = dst if dst is not None else cx_.sb(name, [128, kc, ncols], BF16)
        src = src_dt.view(base, [[row_stride, 128], [128 * row_stride, kc], [1, ncols]])
        P.dma("gpsimd", lambda e: e.dma_start(out=tl.t[:, 0:kc, 0:ncols], in_=src), tl.b, reads=[src_dt.b], writes=[tl.b])
        return tl

    cm = Cx(nc, P)
    nrm = NormT(cm, idb)
    gmem = load_gain_fm(cm, dr["ln_mem"], l)
    memT = cm.sb("memT", [128, 16, 256], BF16)
    mx = [cm.sb("mx", [128, D], F32) for _ in range(2)]
    for mt in range(2):
        P.dma("sync", lambda e, mt=mt: e.dma_start(out=mx[mt].t[:], in_=mem.ap[mt * 128:(mt + 1) * 128, :]), mx[mt].b, reads=[mem.b], writes=[mx[mt].b])
        nrm.run(mx[mt], gmem, memT, mt * 128)
    psm = [cm.ps("psm", [128, 512], F32) for _ in range(2)]
    wk = wload(cm, "wk", dr["w_xkv"], l * D * 1024, 1024, 16, 512)
    wv = wload(cm, "wv", dr["w_xkv"], l * D * 1024 + 512, 1024, 16, 512)
    for h in range(4):
        ps = psm[h % 2]
        for kc in range(16):
            P.op("tensor", lambda e, ps=ps, h=h, kc=kc: e.matmul(ps.t[:, 0:256], lhsT=wk.t[:, kc, h * 128:(h + 1) * 128], rhs=memT.t[:, kc, :], start=(kc == 0), stop=(kc == 15)),
                 reads=[wk.b, memT.b], writes=[ps.b])
        P.op("vector", lambda e, ps=ps, h=h: e.tensor_copy(out=KmT.t[:, h, :], in_=ps.t[:, 0:256]), reads=[ps.b], writes=[KmT.b])
    for mt in range(2):
        ps = psm[mt % 2]
        for kc in range(16):
            P.op("tensor", lambda e, ps=ps, mt=mt, kc=kc: e.matmul(ps.t[:], lhsT=memT.t[:, kc, mt * 128:(mt + 1) * 128], rhs=wv.t[:, kc, :], start=(kc == 0), stop=(kc == 15)),
                 reads=[wv.b, memT.b], writes=[ps.b])
        P.op("vector", lambda e, ps=ps, mt=mt: e.tensor_copy(out=Vm.t[:, mt, :], in_=ps.t[:]), reads=[ps.b], writes=[Vm.b])
    cm.close()
    if C_STOP == 1:
        P.disabled = True

    for blk in range(NB):
        t0 = blk * 512
        for tt in range(4):
            P.dma("sync", lambda e, tt=tt, t0=t0: e.dma_start(out=xb.t[:, tt, :], in_=x.ap[t0 + tt * 128:t0 + (tt + 1) * 128, :]), xb.b, reads=[x.b], writes=[xb.b])
        c1 = Cx(nc, P)
        gbc = load_gain_bc(c1, dr["ln_mix_post"], l)
        pTb = c1.sb("pTb", [128, 8, 512], BF16)
        aTb = c1.sb("aTb", [128, 8, 512], BF16)
        y1T = c1.sb("y1T", [128, 16, 512], BF16)
        psc = [c1.ps("psc", [128, 512], F32) for _ in range(6)]
        c1a = Cx(nc, P)
        poolM = c1a.sb("poolM", [128, 2, 4, 128], F32)
        P.dma("sync", lambda e: e.dma_start(out=poolM.t[:], in_=dr["poolM"].ap.rearrange("f s w t -> s f w t")), poolM.b, reads=[dr["poolM"].b], writes=[poolM.b])
        poolHA = c1a.sb("poolHA", [128, 4, 128], F32)
        P.dma("sync", lambda e: e.dma_start(out=poolHA.t[:], in_=dr["poolHA"].ap), poolHA.b, reads=[dr["poolHA"].b], writes=[poolHA.b])
        poolHB = c1a.sb("poolHB", [16, 4, 128], F32)
        P.dma("sync", lambda e: e.dma_start(out=poolHB.t[:], in_=dr["poolHB"].ap), poolHB.b, reads=[dr["poolHB"].b], writes=[poolHB.b])
        ut = [c1a.sb("ut", [128, 1024], F32) for _ in range(2)]
        hA = [c1a.sb("hA", [128, 1024], F32) for _ in range(2)]
        hB = [c1a.sb("hB", [16, 1024], F32) for _ in range(1)]
        yT = c1a.sb("yT", [128, 8, 512], BF16)
        kp = [0]

        def nps():
            p = psc[kp[0] % 6]
            kp[0] += 1
            return p
        P.dma("sync", lambda e, t0=t0: e.dma_start(out=aTb.t[:], in_=aT.ap[:, :, t0:t0 + 512].rearrange("h d t -> (h d) t").rearrange("(k p) t -> p k t", p=128)),
              aTb.b, reads=[aT.b], writes=[aTb.b])
        for tt in range(4):
            i = blk * 4 + tt
            u_, ha, hb = ut[tt % 2], hA[tt % 2], hB[0]
            P.dma("sync", lambda e, i=i, u_=u_: e.dma_start(out=u_.t[:], in_=u.ap[i * 128:(i + 1) * 128, :]), u_.b, reads=[u.b], writes=[u_.b])
            for r in range(8):
                P.dma("sync", lambda e, i=i, r=r, ha=ha: e.dma_start(out=ha.t[r * 16:(r + 1) * 16, :], in_=uhg.ap[r, i, :, :]), ha.b, reads=[uhg.b], writes=[ha.b])
            if i >= 1:
                P.dma("sync", lambda e, i=i, hb=hb: e.dma_start(out=hb.t[:], in_=uhg.ap[7, i - 1, :, :]), hb.b, reads=[uhg.b], writes=[hb.b])
            f = 0 if i == 0 else 1
            ps = nps()
            ps2 = nps()
            for cc in range(8):
                gi = cc // 2
                pp = ps if cc < 4 else ps2
                o = pp.t[:, (cc % 4) * 128:(cc % 4 + 1) * 128]
                P.op("tensor", lambda e, o=o, cc=cc, gi=gi, u_=u_, f=f: e.matmul(o, lhsT=u_.t[:, cc * 128:(cc + 1) * 128], rhs=poolM.t[:, f, gi, :], start=True, stop=False),
                     reads=[u_.b, poolM.b], writes=[pp.b])
                P.op("tensor", lambda e, o=o, cc=cc, gi=gi, ha=ha, i=i: e.matmul(o, lhsT=ha.t[:, cc * 128:(cc + 1) * 128], rhs=poolHA.t[:, gi, :], start=False, stop=(i == 0)),
                     reads=[ha.b, poolHA.b], writes=[pp.b])
                if i >= 1:
                    P.op("tensor", lambda e, o=o, cc=cc, gi=gi, hb=hb: e.matmul(o, lhsT=hb.t[:, cc * 128:(cc + 1) * 128], rhs=poolHB.t[:, gi, :], start=False, stop=True),
                         reads=[hb.b, poolHB.b], writes=[pp.b])
            P.op("vector", lambda e, ps=ps, tt=tt: e.tensor_copy(out=yT.t[:, 0:4, tt * 128:(tt + 1) * 128], in_=ps.t[:].rearrange("p (c t) -> p c t", c=4)), reads=[ps.b], writes=[yT.b])
            P.op("scalar", lambda e, ps2=ps2, tt=tt: e.copy(out=yT.t[:, 4:8, tt * 128:(tt + 1) * 128], in_=ps2.t[:].rearrange("p (c t) -> p c t", c=4)), reads=[ps2.b], writes=[yT.b])
        wp = c1a.sb("wp", [128, 8, 256], BF16)
        P.dma("gpsimd", lambda e: e.dma_start(out=wp.t[:], in_=dr["w_pool"].view(l * 4 * 256 * 256, [[256, 128], [128 * 256, 8], [1, 256]])), wp.b, reads=[dr["w_pool"].b], writes=[wp.b])
        psc_fm = c1a.sb("pscale", [128, 8], F32)
        P.dma("sync", lambda e: e.dma_start(out=psc_fm.t[:], in_=dr["pool_scale"].view(l * 1024, [[1, 128], [128, 8]]), allow_slow_non_contiguous=True),
              psc_fm.b, reads=[dr["pool_scale"].b], writes=[psc_fm.b])
        for gi in range(4):
            for oc in range(2):
                ps = nps()
                for ci in range(2):
                    P.op("tensor", lambda e, ps=ps, gi=gi, oc=oc, ci=ci: e.matmul(ps.t[:], lhsT=wp.t[:, gi * 2 + ci, oc * 128:(oc + 1) * 128], rhs=yT.t[:, gi * 2 + ci, :],
                                                                              start=(ci == 0), stop=(ci == 1)), reads=[wp.b, yT.b], writes=[ps.b])
                P.op("vector", lambda e, ps=ps, gi=gi, oc=oc: e.tensor_scalar(out=pTb.t[:, gi * 2 + oc, :], in0=ps.t[:], scalar1=psc_fm.t[:, gi * 2 + oc:gi * 2 + oc + 1], scalar2=None, op0=ALU.mult),
                     reads=[ps.b, psc_fm.b], writes=[pTb.b])
        c1a.close()
        if C_STOP == 2:
            P.disabled = True
        wa = [c1.sb("wa", [128, 8, 256], BF16) for _ in range(2)]
        wpb = [c1.sb("wpb", [128, 8, 256], BF16) for _ in range(2)]
        gat = [c1.sb("gat", [128, 512], F32) for _ in range(2)]
        gpt = [c1.sb("gpt", [128, 512], F32) for _ in range(2)]
        t1 = [c1.sb("t1", [128, 512], F32) for _ in range(2)]
        t2 = [c1.sb("t2", [128, 512], F32) for _ in range(2)]
        for c4 in range(8):
            wa_ = wload(c1, None, dr["w_br_attn"], l * 1024 * D + c4 * 256, D, 8, 256, dst=wa[c4 % 2])
            wp_ = wload(c1, None, dr["w_br_pool"], l * 1024 * D + c4 * 256, D, 8, 256, dst=wpb[c4 % 2])
            for o4 in range(2):
                oc = c4 * 2 + o4
                k2 = oc % 2
                P.dma("sync", lambda e, oc=oc, k2=k2, t0=t0: e.dma_start(out=gat[k2].t[:], in_=gT.ap[oc * 128:(oc + 1) * 128, t0:t0 + 512]), gat[k2].b, reads=[gT.b], writes=[gat[k2].b])
                P.dma("sync", lambda e, oc=oc, k2=k2, t0=t0: e.dma_start(out=gpt[k2].t[:], in_=gT.ap[2048 + oc * 128:2048 + (oc + 1) * 128, t0:t0 + 512]), gpt[k2].b, reads=[gT.b], writes=[gpt[k2].b])
                pa = nps()
                for kc in range(8):
                    P.op("tensor", lambda e, pa=pa, kc=kc, o4=o4, wa_=wa_: e.matmul(pa.t[:], lhsT=wa_.t[:, kc, o4 * 128:(o4 + 1) * 128], rhs=aTb.t[:, kc, :], start=(kc == 0), stop=(kc == 7)),
                         reads=[wa_.b, aTb.b], writes=[pa.b])
                pq = nps()
                for kc in range(8):
                    P.op("tensor", lambda e, pq=pq, kc=kc, o4=o4, wp_=wp_: e.matmul(pq.t[:], lhsT=wp_.t[:, kc, o4 * 128:(o4 + 1) * 128], rhs=pTb.t[:, kc, :], start=(kc == 0), stop=(kc == 7)),
                         reads=[wp_.b, pTb.b], writes=[pq.b])
                P.op("vector", lambda e, pa=pa, k2=k2: e.tensor_tensor(out=t1[k2].t[:], in0=pa.t[:], in1=gat[k2].t[:], op=ALU.mult), reads=[pa.b, gat[k2].b], writes=[t1[k2].b])
                P.op("vector", lambda e, pq=pq, k2=k2: e.tensor_tensor(out=t2[k2].t[:], in0=pq.t[:], in1=gpt[k2].t[:], op=ALU.mult), reads=[pq.b, gpt[k2].b], writes=[t2[k2].b])
                P.op("gpsimd", lambda e, oc=oc, k2=k2: e.tensor_tensor(out=y1T.t[:, oc, :], in0=t1[k2].t[:], in1=t2[k2].t[:], op=ALU.add), reads=[t1[k2].b, t2[k2].b], writes=[y1T.b])
        wm = [c1.sb("wm", [128, 16, 256], BF16) for _ in range(2)]
        for c4 in range(8):
            wm_ = wload(c1, None, dr["w_mix_out"], l * D * D + c4 * 256, D, 16, 256, dst=wm[c4 % 2])
            for tt in range(4):
                ps = nps()
                for kc in range(16):
                    P.op("tensor", lambda e, ps=ps, kc=kc, tt=tt, wm_=wm_: e.matmul(ps.t[:, 0:256], lhsT=y1T.t[:, kc, tt * 128:(tt + 1) * 128], rhs=wm_.t[:, kc, :], start=(kc == 0), stop=(kc == 15)),
                         reads=[wm_.b, y1T.b], writes=[ps.b])
                P.op("scalar", lambda e, ps=ps, tt=tt, c4=c4: e.copy(out=ysb[tt].t[:, c4 * 256:(c4 + 1) * 256], in_=ps.t[:, 0:256]), reads=[ps.b], writes=[ysb[tt].b])
        for tt in range(4):
            post_norm_residual(c1, ysb[tt], xb.t[:, tt, :], xb, gbc, scr)
        c1.close()
        if C_STOP == 3:
            P.disabled = True
        c2 = Cx(nc, P)
        gbc = load_gain_bc(c2, dr["ln_x_post"], l)
        gfm = load_gain_fm(c2, dr["ln_x_pre"], l)
        nrm = NormT(c2, idb)
        hT = c2.sb("hT", [128, 16, 512], BF16)
        xtl = [T(None, "x") for _ in range(4)]
        for tt in range(4):
            xv = T(None, "xv")
            xv.t = _View(xb.t[:, tt, :])
            xv.b = xb.b
            nrm.run(xv, gfm, hT, tt * 128)
        psc = [c2.ps("psc2", [128, 512], F32) for _ in range(6)]
        kp = [0]
        wq = wload(c2, "wq", dr["w_xq"], l * D * 512, 512, 16, 512)
        wo = wload(c2, "wo", dr["w_xo"], l * 512 * D, D, 4, D)
        qxT = c2.sb("qxT", [128, 4, 512], BF16)
        oxT = c2.sb("oxT", [128, 4, 512], BF16)
        Ex = [c2.sb("Ex", [128, 512], BF16) for _ in range(4)]
        rden = c2.sb("rden", [128, 512], F32)
        for h in range(4):
            ps = nps2 = psc[kp[0] % 6]
            kp[0] += 1
            for kc in range(16):
                P.op("tensor", lambda e, ps=ps, kc=kc, h=h: e.matmul(ps.t[:], lhsT=wq.t[:, kc, h * 128:(h + 1) * 128], rhs=hT.t[:, kc, :], start=(kc == 0), stop=(kc == 15)),
                     reads=[wq.b, hT.b], writes=[ps.b])
            P.op("vector", lambda e, ps=ps, h=h: e.tensor_copy(out=qxT.t[:, h, :], in_=ps.t[:]), reads=[ps.b], writes=[qxT.b])
        for h in range(4):
            es = []
            for mt in range(2):
                ps = psc[kp[0] % 6]
                kp[0] += 1
                E = Ex[(h * 2 + mt) % 4]
                P.op("tensor", lambda e, ps=ps, h=h, mt=mt: e.matmul(ps.t[:], lhsT=KmT.t[:, h, mt * 128:(mt + 1) * 128], rhs=qxT.t[:, h, :], start=True, stop=True),
                     reads=[KmT.b, qxT.b], writes=[ps.b])
                P.op("scalar", lambda e, ps=ps, E=E: e.activation(out=E.t[:], in_=ps.t[:], func=AF.Exp, scale=128 ** -0.5), reads=[ps.b], writes=[E.b])
                es.append(E)
            po = psc[kp[0] % 6]
            kp[0] += 1
            pd = psc[kp[0] % 6]
            kp[0] += 1
            for mt in range(2):
                P.op("tensor", lambda e, po=po, h=h, mt=mt, E=es[mt]: e.matmul(po.t[:], lhsT=Vm.t[:, mt, h * 128:(h + 1) * 128], rhs=E.t[:], start=(mt == 0), stop=(mt == 1)),
                     reads=[Vm.b, E.b], writes=[po.b])
            for mt in range(2):
                P.op("tensor", lambda e, pd=pd, mt=mt, E=es[mt]: e.matmul(pd.t[:], lhsT=ones_b.t[:], rhs=E.t[:], start=(mt == 0), stop=(mt == 1)),
                     reads=[ones_b.b, E.b], writes=[pd.b])
            P.op("vector", lambda e, pd=pd: e.reciprocal(out=rden.t[:], in_=pd.t[:]), reads=[pd.b], writes=[rden.b])
            P.op("vector", lambda e, po=po, h=h: e.tensor_tensor(out=oxT.t[:, h, :], in0=po.t[:], in1=rden.t[:], op=ALU.mult), reads=[po.b, rden.b], writes=[oxT.b])
        for c4 in range(4):
            for tt in range(4):
                ps = psc[kp[0] % 6]
                kp[0] += 1
                for h in range(4):
                    P.op("tensor", lambda e, ps=ps, h=h, tt=tt, c4=c4: e.matmul(ps.t[:], lhsT=oxT.t[:, h, tt * 128:(tt + 1) * 128], rhs=wo.t[:, h, c4 * 512:(c4 + 1) * 512], start=(h == 0), stop=(h == 3)),
                         reads=[wo.b, oxT.b], writes=[ps.b])
                P.op("scalar", lambda e, ps=ps, tt=tt, c4=c4: e.copy(out=ysb[tt].t[:, c4 * 512:(c4 + 1) * 512], in_=ps.t[:]), reads=[ps.b], writes=[ysb[tt].b])
        for tt in range(4):
            post_norm_residual(c2, ysb[tt], xb.t[:, tt, :], xb, gbc, scr)
        c2.close()
        if C_STOP == 4:
            P.disabled = True
        c3 = Cx(nc, P)
        gbc = load_gain_bc(c3, dr["ln_ffn_post"], l)
        hT = c3.sb("hT", [128, 16, 512], BF16)
        hid = c3.sb("hid", [128, NFF, 512], BF16)
        cn = Cx(nc, P)
        gfm = load_gain_fm(cn, dr["ln_ffn_pre"], l)
        nrm = NormT(cn, idb)
        for tt in range(4):
            xv = T(None, "xv")
            xv.t = _View(xb.t[:, tt, :])
            xv.b = xb.b
            nrm.run(xv, gfm, hT, tt * 128)
        cn.close()
        ca = Cx(nc, P)
        wg = [ca.sb("wg", [128, 16, 128], BF16) for _ in range(3)]
        wu = [ca.sb("wu", [128, 16, 128], BF16) for _ in range(3)]
        pg = [ca.ps("pg", [128, 512], F32) for _ in range(2)]
        pu = [ca.ps("pu", [128, 512], F32) for _ in range(2)]
        sil = [ca.sb("sil", [128, 512], F32) for _ in range(2)]
        for f in range(NFF):
            wg_ = wload(ca, None, dr["w_gate"], l * D * D_FF + f * 128, D_FF, 16, 128, dst=wg[f % 3])
            wu_ = wload(ca, None, dr["w_up"], l * D * D_FF + f * 128, D_FF, 16, 128, dst=wu[f % 3])
            g_, u_ = pg[f % 2], pu[f % 2]
            for kc in range(16):
                P.op("tensor", lambda e, g_=g_, kc=kc, wg_=wg_: e.matmul(g_.t[:], lhsT=wg_.t[:, kc, :], rhs=hT.t[:, kc, :], start=(kc == 0), stop=(kc == 15)), reads=[wg_.b, hT.b], writes=[g_.b])
            for kc in range(16):
                P.op("tensor", lambda e, u_=u_, kc=kc, wu_=wu_: e.matmul(u_.t[:], lhsT=wu_.t[:, kc, :], rhs=hT.t[:, kc, :], start=(kc == 0), stop=(kc == 15)), reads=[wu_.b, hT.b], writes=[u_.b])
            s_ = sil[f % 2]
            P.op("scalar", lambda e, g_=g_, s_=s_: e.activation(out=s_.t[:], in_=g_.t[:], func=AF.Silu), reads=[g_.b], writes=[s_.b])
            P.op("vector", lambda e, u_=u_, s_=s_, f=f: e.tensor_tensor(out=hid.t[:, f, :], in0=u_.t[:], in1=s_.t[:], op=ALU.mult), reads=[u_.b, s_.b], writes=[hid.b])
        ca.close()
        if C_STOP == 5:
            P.disabled = True
        cb = Cx(nc, P)
        wd = [cb.sb("wd", [128, 4, 512], BF16) for _ in range(3)]
        py = [cb.ps("py", [128, 512], F32) for _ in range(4)]
        kw_ = 0
        for q4 in range(4):
            for f4 in range(NFF // 4):
                wd_ = wd[kw_ % 3]
                kw_ += 1
                src = dr["w_down"].view(l * D_FF * D + f4 * 512 * D + q4 * 512, [[D, 128], [128 * D, 4], [1, 512]])
                P.dma("gpsimd", lambda e, wd_=wd_, src=src: e.dma_start(out=wd_.t[:], in_=src), wd_.b, reads=[dr["w_down"].b], writes=[wd_.b])
                for fi in range(4):
                    f = f4 * 4 + fi
                    for tt in range(4):
                        ps = py[tt]
                        P.op("tensor", lambda e, ps=ps, f=f, fi=fi, tt=tt, wd_=wd_: e.matmul(ps.t[:], lhsT=hid.t[:, f, tt * 128:(tt + 1) * 128], rhs=wd_.t[:, fi, :],
                                                                                       start=(f == 0), stop=(f == NFF - 1)), reads=[wd_.b, hid.b], writes=[ps.b])
            for tt in range(4):
                ps = py[tt]
                P.op("scalar", lambda e, ps=ps, tt=tt, q4=q4: e.copy(out=ysb[tt].t[:, q4 * 512:(q4 + 1) * 512], in_=ps.t[:]), reads=[ps.b], writes=[ysb[tt].b])
        for tt in range(4):
            post_norm_residual(cb, ysb[tt], xb.t[:, tt, :], xb, gbc, scr)
        cb.close()
        c3.close()
        P.disabled = False
        for tt in range(4):
            P.dma("sync", lambda e, tt=tt, t0=t0: e.dma_start(out=xo.ap[t0 + tt * 128:t0 + (tt + 1) * 128, :], in_=xb.t[:, tt, :]), xb.b, reads=[xb.b], writes=[xo.b])
    cxo.close()


class _View:
    def __init__(self, ap):
        self.ap = ap

    def __getitem__(self, key):
        return self.ap[key]


NT_FULL = 16
DEPTH = 4
_PROGS = {}


def _mk(nc, dr, name, shape, dt, kind="ExternalInput"):
    dr[name] = DT(nc, name, shape, dt, kind)


def declare_A(nc, dr, NT, L, kind_out):
    NTOK = NT * 128
    _mk(nc, dr, "x", [NTOK, D], F32)
    _mk(nc, dr, "w_in", [L, D, IN_WIDTH], F32)
    _mk(nc, dr, "ln_mix_pre", [L, D], F32)
    _mk(nc, dr, "ident", [128, 128], F32)
    for name, shape, dt in (("qT", [16, 64, NTOK], BF16), ("kT", [3, 4, 64, NTOK], BF16), ("vcT", [4, 64, NTOK], BF16),
                            ("v", [2, NTOK, 256], BF16), ("gl", [NTOK, 48], F32), ("u", [NTOK, 1024], F32), ("gT", [4096, NTOK], F32)):
        _mk(nc, dr, name, shape, dt, kind_out)


def declare_B_tables(nc, dr, NT):
    SEQ = NT * 8 * 128
    n_sel = SEQ // 64
    NNT = SEQ // 2048
    for name, shape in (("jrev", [128, 128]), ("wstrip", [128, 8192]), ("selall", [48, 3072]), ("ovl", [128, NNT, n_sel]),
                        ("keepS", [128, 2 * n_sel]), ("addS", [128, 2 * n_sel]), ("oh_sel", [33, LEN_S]), ("oh_win", [33, LEN_W]), ("oh_cmp", [33, LEN_C])):
        _mk(nc, dr, name, shape, F32)
    _mk(nc, dr, "gd_sel", [16, LEN_S], BF16, "Internal")
    _mk(nc, dr, "gd_win", [16, LEN_W], BF16, "Internal")
    _mk(nc, dr, "gd_cmp", [16, LEN_C], BF16, "Internal")


def declare_B_w(nc, dr, L):
    _mk(nc, dr, "rel_bias", [16, 32], F32)
    _mk(nc, dr, "cmp_pe", [L, 2, 32, 64], F32)
    _mk(nc, dr, "cmp_w1", [L, 2, 2048, 128], F32)
    _mk(nc, dr, "cmp_w2", [L, 2, 128, 64], F32)


def declare_C_w(nc, dr, L):
    _mk(nc, dr, "mem", [256, D], F32)
    for n in ("ln_mix_post", "ln_x_pre", "ln_x_post", "ln_mem", "ln_ffn_pre", "ln_ffn_post"):
        _mk(nc, dr, n, [L, D], F32)
    for n, shape in (("w_pool", [L, 4, 256, 256]), ("pool_scale", [L, 1024]), ("w_br_attn", [L, 1024, D]), ("w_br_pool", [L, 1024, D]),
                     ("w_mix_out", [L, D, D]), ("w_xq", [L, D, 512]), ("w_xkv", [L, D, 1024]), ("w_xo", [L, 512, D]),
                     ("w_gate", [L, D, D_FF]), ("w_up", [L, D, D_FF]), ("w_down", [L, D_FF, D])):
        _mk(nc, dr, n, shape, F32)
    for n, shape in (("poolM", [2, 128, 4, 128]), ("poolHA", [128, 4, 128]), ("poolHB", [16, 4, 128])):
        _mk(nc, dr, n, shape, F32)


def build_prog_A(NT):
    nc = bass.Bass("TRN2", target_bir_lowering=False)
    dr = {}
    declare_A(nc, dr, NT, 1, "ExternalOutput")
    with ExitStack() as st:
        P = Prog(nc, st)
        phase_A(nc, P, NT, 0, dr)
    return nc


def build_prog_B(NT):
    NTOK = NT * 128
    nc = bass.Bass("TRN2", target_bir_lowering=False)
    dr = {}
    _mk(nc, dr, "qT", [16, 64, NTOK], BF16)
    _mk(nc, dr, "gl", [NTOK, 48], F32)
    _mk(nc, dr, "kTg", [8, 3, 4, 64, NTOK], BF16)
    _mk(nc, dr, "vcTg", [8, 4, 64, NTOK], BF16)
    _mk(nc, dr, "vg", [8, 2, NTOK, 256], BF16)
    _mk(nc, dr, "ident", [128, 128], F32)
    declare_B_w(nc, dr, 1)
    declare_B_tables(nc, dr, NT)
    _mk(nc, dr, "aT", [16, 64, NTOK], BF16, "ExternalOutput")
    with ExitStack() as st:
        P = Prog(nc, st)
        cx, env = phase_B(nc, P, NT, 0, dr)
        phase_B2(nc, P, NT, 0, dr, cx, env)
    return nc


def build_prog_C(NT):
    NTOK = NT * 128
    nc = bass.Bass("TRN2", target_bir_lowering=False)
    dr = {}
    _mk(nc, dr, "x", [NTOK, D], F32)
    _mk(nc, dr, "aT", [16, 64, NTOK], BF16)
    _mk(nc, dr, "u", [NTOK, 1024], F32)
    _mk(nc, dr, "uhg", [8, NT, 16, 1024], F32)
    _mk(nc, dr, "gT", [4096, NTOK], F32)
    _mk(nc, dr, "ident", [128, 128], F32)
    declare_C_w(nc, dr, 1)
    _mk(nc, dr, "x_out", [NTOK, D], F32, "ExternalOutput")
    with ExitStack() as st:
        P = Prog(nc, st)
        phase_C(nc, P, NT, 0, dr)
    return nc


C_W_NAMES = ("ln_mix_post", "ln_x_pre", "ln_x_post", "ln_mem", "ln_ffn_pre", "ln_ffn_post", "w_pool", "pool_scale", "w_br_attn", "w_br_pool",
             "w_mix_out", "w_xq", "w_xkv", "w_xo", "w_gate", "w_up", "w_down")


def run_model(inputs, NT, depth):
    f32 = np.float32
    inp = {k: np.ascontiguousarray(np.asarray(v)) for k, v in inputs.items()}
    NTOK = NT * 128
    S = NTOK * 8
    key = ("unfused", NT)
    if key not in _PROGS:
        _PROGS[key] = (build_prog_A(NT), build_prog_B(NT), build_prog_C(NT))
    pA, pB, pC = _PROGS[key]
    loc = [np.concatenate([np.arange((8 * i + c) * 128, (8 * i + c + 1) * 128) for i in range(NT)]) for c in range(8)]
    xg = inp["x"][0]
    x_loc = [np.ascontiguousarray(xg[loc[c]]) for c in range(8)]
    sh = shared_tables(NT)
    ctab = [dict(core_tables(c, NT), **pool_tables(c)) for c in range(8)]
    cores = list(range(8))
    mem = inp["mem"][0]
    for l in range(depth):
        insA = [{"x": x_loc[c], "w_in": inp["w_in"][l:l + 1], "ln_mix_pre": inp["ln_mix_pre"][l:l + 1], "ident": sh["ident"]} for c in cores]
        rA = run_bass_kernel_spmd(pA, insA, core_ids=cores).results
        kTg = np.stack([np.asarray(rA[c]["kT"]) for c in cores])
        vcTg = np.stack([np.asarray(rA[c]["vcT"]) for c in cores])
        vg = np.stack([np.asarray(rA[c]["v"]) for c in cores])
        uhg = np.stack([np.asarray(rA[c]["u"]).reshape(NT, 128, 1024)[:, 112:, :] for c in cores])
        insB = []
        for c in cores:
            m = {"qT": rA[c]["qT"], "gl": rA[c]["gl"], "kTg": kTg, "vcTg": vcTg, "vg": vg, "ident": sh["ident"], "rel_bias": inp["rel_bias"],
                 "cmp_pe": inp["cmp_pe"][l:l + 1], "cmp_w1": inp["cmp_w1"][l:l + 1], "cmp_w2": inp["cmp_w2"][l:l + 1]}
            for k in ("jrev", "wstrip", "selall", "ovl"):
                m[k] = sh[k]
            for k in ("keepS", "addS", "oh_sel", "oh_win", "oh_cmp"):
                m[k] = ctab[c][k]
            insB.append(m)
        rB = run_bass_kernel_spmd(pB, insB, core_ids=cores).results
        insC = []
        for c in cores:
            m = {"x": x_loc[c], "aT": rB[c]["aT"], "u": rA[c]["u"], "uhg": uhg, "gT": rA[c]["gT"], "ident": sh["ident"], "mem": mem,
                 "poolM": ctab[c]["poolM"], "poolHA": ctab[c]["poolHA"], "poolHB": ctab[c]["poolHB"]}
            for k in C_W_NAMES:
                m[k] = inp[k][l:l + 1]
            insC.append(m)
        rC = run_bass_kernel_spmd(pC, insC, core_ids=cores).results
        x_loc = [np.asarray(rC[c]["x_out"]) for c in cores]
    out = np.empty((S, D), f32)
    for c in cores:
        out[loc[c]] = x_loc[c]
    return out[None]


def kernel(**inputs):
    return run_model(inputs, NT_FULL, DEPTH)
```

```python
import numpy as np
import ml_dtypes
from contextlib import ExitStack
import concourse.bass as bass
import concourse.mybir as mybir
from concourse.bass_utils import run_bass_kernel_spmd

F32 = mybir.dt.float32
BF16 = mybir.dt.bfloat16
AF = mybir.ActivationFunctionType
ALU = mybir.AluOpType
AX = mybir.AxisListType

ENGS = ("tensor", "vector", "scalar", "gpsimd", "sync")
NCORES = 8
D = 2048
NEG = -30000.0


class Buf:
    __slots__ = ("name", "writer", "readers", "dsem")

    def __init__(self, name):
        self.name = name
        self.writer = None
        self.readers = []
        self.dsem = None


class Prog:
    def __init__(self, nc, stack):
        self.nc = nc
        self.stack = stack
        self.ops = {e: [] for e in ENGS}
        self.sems = {}
        self.count = {}
        self.known = {e: {} for e in ENGS}
        for e in ENGS:
            self._mksem("E_" + e)
        self.n_dsem = 0
        self.dsem_pool = []

    def _mksem(self, key):
        self.sems[key] = self.stack.enter_context(self.nc.semaphore(key))
        self.count[key] = 0

    def dsem_of(self, buf):
        if buf.dsem is None:
            if self.dsem_pool:
                key = self.dsem_pool.pop()
            else:
                key = "D%d" % self.n_dsem
                self.n_dsem += 1
                self._mksem(key)
            buf.dsem = key
        return buf.dsem

    def _deps(self, eng, reads, writes):
        need = {}

        def add(ev):
            if ev is None:
                return
            k, v = ev
            if eng == "tensor" and k == "E_tensor":
                return
            if need.get(k, 0) < v:
                need[k] = v
        for b in reads:
            add(b.writer)
        for b in writes:
            add(b.writer)
            for r in b.readers:
                add(r)
        waits = []
        kn = self.known[eng]
        for k, v in need.items():
            if kn.get(k, 0) < v:
                kn[k] = v
                waits.append((k, v))
        return waits

    def _commit(self, ev, reads, writes):
        for b in reads:
            b.readers.append(ev)
            if len(b.readers) > 16:
                mx = {}
                for k, v in b.readers:
                    if mx.get(k, 0) < v:
                        mx[k] = v
                b.readers = list(mx.items())
        for b in writes:
            b.writer = ev
            b.readers = []

    disabled = False

    def op(self, eng, fn, reads=(), writes=()):
        if self.disabled:
            return None
        waits = self._deps(eng, reads, writes)
        key = "E_" + eng
        self.count[key] += 1
        ev = (key, self.count[key])
        self.ops[eng].append((waits, fn, (key, 1)))
        self._commit(ev, reads, writes)
        return ev

    def dma(self, eng, fn, sbuf, reads=(), writes=(), inc=16):
        if self.disabled:
            return None
        waits = self._deps(eng, reads, writes)
        key = self.dsem_of(sbuf)
        self.count[key] += inc
        ev = (key, self.count[key])
        self.ops[eng].append((waits, fn, (key, inc)))
        self._commit(ev, reads, writes)
        return ev

    def emit(self):
        nc = self.nc
        sems = self.sems
        ops = self.ops
        fw = {k: v for k, v in self.count.items() if v > 0}
        with nc.Block() as block:
            def runner(ename):
                def run(eng):
                    for waits, fn, inc in ops[ename]:
                        for k, v in waits:
                            eng.wait_ge(sems[k], v)
                        ins = fn(eng)
                        ins.then_inc(sems[inc[0]], inc[1])
                    for k, v in fw.items():
                        eng.wait_ge(sems[k], v)
                return run
            block.tensor(runner("tensor"))
            block.vector(runner("vector"))
            block.scalar(runner("scalar"))
            block.gpsimd(runner("gpsimd"))
            block.sync(runner("sync"))
        for e in ENGS:
            self.ops[e] = []
            for k, v in fw.items():
                self.known[e][k] = v


class T:
    __slots__ = ("t", "b")

    def __init__(self, t, name):
        self.t = t
        self.b = Buf(name)


_UID = [0]


class Cx:
    def __init__(self, nc, P):
        self.nc = nc
        self.P = P
        self.st = ExitStack()
        self.n = 0
        self.tiles = []

    def sb(self, name, shape, dt):
        _UID[0] += 1
        nm = "%s_%d" % (name, _UID[0])
        t = T(self.st.enter_context(self.nc.sbuf_tensor(nm, shape, dt)), nm)
        self.tiles.append(t)
        return t

    def ps(self, name, shape, dt=F32):
        _UID[0] += 1
        nm = "%s_%d" % (name, _UID[0])
        return T(self.st.enter_context(self.nc.psum_tensor(nm, shape, dt)), nm)

    def close(self):
        self.P.emit()
        self.st.close()
        for t in self.tiles:
            if t.b.dsem is not None:
                self.P.dsem_pool.append(t.b.dsem)
                t.b.dsem = None
        self.tiles = []


class DT:
    def __init__(self, nc, name, shape, dt, kind):
        self.h = nc.dram_tensor(name, list(shape), dt, kind=kind)
        self.ap = self.h.ap()
        self.b = Buf(name)
        self.shape = list(shape)

    def view(self, offset, ap):
        return bass.AP(tensor=self.h, offset=offset, ap=ap)


def dq(k):
    return ("sync", "scalar")[k % 2]

def load_gain_fm(cx, vec_ap_1d_handle, row):
    P = cx.P
    g = cx.sb("gfm", [128, 16], F32)
    src = vec_ap_1d_handle.view(row * D, [[1, 128], [128, 16]])
    P.dma("sync", lambda e: e.dma_start(out=g.t[:], in_=src, allow_slow_non_contiguous=True), g.b,
          reads=[vec_ap_1d_handle.b], writes=[g.b])
    return g


def load_gain_bc(cx, dth, row, n=D):
    P = cx.P
    g = cx.sb("gbc", [128, n], F32)
    src = dth.view(row * n, [[0, 128], [1, n]])
    P.dma("sync", lambda e: e.dma_start(out=g.t[:], in_=src), g.b, reads=[dth.b], writes=[g.b])
    return g


class NormT:
    def __init__(self, cx, identb):
        self.cx = cx
        self.identb = identb
        self.junk = cx.sb("nt_junk", [128, D], F32)
        self.ssq = [cx.sb("nt_ssq", [128, 1], F32) for _ in range(2)]
        self.rstd = [cx.sb("nt_rstd", [128, 1], F32) for _ in range(2)]
        self.xn = [cx.sb("nt_xn", [128, D], BF16) for _ in range(2)]
        self.pT = [cx.ps("nt_pT", [128, 4, 128], BF16) for _ in range(2)]
        self.k = 0
        self.kp = 0

    def run(self, xt, gfm, hT, tok0, eps=1e-6, ncols=D):
        P = self.cx.P
        k = self.k % 2
        self.k += 1
        junk, ssq, rstd, xn = self.junk, self.ssq[k], self.rstd[k], self.xn[k]
        P.op("scalar", lambda e: e.activation(out=junk.t[:, 0:ncols], in_=xt.t[:, 0:ncols], func=AF.Square, accum_out=ssq.t[:]),
             reads=[xt.b], writes=[junk.b, ssq.b])
        P.op("scalar", lambda e: e.activation(out=rstd.t[:], in_=ssq.t[:], func=AF.Sqrt, scale=1.0 / ncols, bias=eps),
             reads=[ssq.b], writes=[rstd.b])
        P.op("vector", lambda e: e.reciprocal(out=rstd.t[:], in_=rstd.t[:]), reads=[rstd.b], writes=[rstd.b])
        P.op("vector", lambda e: e.tensor_scalar(out=xn.t[:, 0:ncols], in0=xt.t[:, 0:ncols], scalar1=rstd.t[:, 0:1], scalar2=None, op0=ALU.mult),
             reads=[xt.b, rstd.b], writes=[xn.b])
        for g4 in range(ncols // 512):
            pT = self.pT[self.kp % 2]
            self.kp += 1
            for j in range(4):
                c = g4 * 4 + j
                P.op("tensor", lambda e, c=c, j=j, pT=pT: e.transpose(out=pT.t[:, j, :], in_=xn.t[:, c * 128:(c + 1) * 128], identity=self.identb.t[:]),
                     reads=[xn.b, self.identb.b], writes=[pT.b])
            P.op("vector", lambda e, g4=g4, pT=pT: e.tensor_tensor(
                out=hT.t[:, g4 * 4:(g4 + 1) * 4, tok0:tok0 + 128], in0=pT.t[:],
                in1=gfm.t[:, g4 * 4:(g4 + 1) * 4].unsqueeze(2).to_broadcast([128, 4, 128]), op=ALU.mult),
                reads=[pT.b, gfm.b], writes=[hT.b])
        return rstd


def make_ident(cx, ident_d):
    P = cx.P
    idf = cx.sb("idf", [128, 128], F32)
    idb = cx.sb("idb", [128, 128], BF16)
    P.dma("sync", lambda e: e.dma_start(out=idf.t[:], in_=ident_d.ap[:, :]), idf.b, reads=[ident_d.b], writes=[idf.b])
    P.op("vector", lambda e: e.tensor_copy(out=idb.t[:], in_=idf.t[:]), reads=[idf.b], writes=[idb.b])
    return idf, idb


ZQ, ZKV, ZGL, ZU, ZML = 0, 1024, 2560, 2608, 3632
IN_WIDTH = 7728


def phase_A(nc, P, NT, l, dr):
    NTOK = NT * 128
    NB = NT // 4
    cx = Cx(nc, P)
    idf, idb = make_ident(cx, dr["ident"])
    gfm = load_gain_fm(cx, dr["ln_mix_pre"], l)
    nrm = NormT(cx, idb)
    hT = [cx.sb("hT", [128, 16, 512], BF16) for _ in range(NB)]
    xin = [cx.sb("xin", [128, D], F32) for _ in range(2)]
    x = dr["x"]
    for t in range(NT):
        xt = xin[t % 2]
        P.dma("sync", lambda e, t=t, xt=xt: e.dma_start(out=xt.t[:], in_=x.ap[t * 128:(t + 1) * 128, :]), xt.b,
              reads=[x.b], writes=[xt.b])
        nrm.run(xt, gfm, hT[t // 4], (t % 4) * 128)
    w_in = dr["w_in"]
    wbuf = [cx.sb("wA", [128, 16, 512], BF16) for _ in range(2)]
    psum = [cx.ps("pA", [128, 512], F32) for _ in range(4)]
    stg = [cx.sb("stgA", [128, 512], F32) for _ in range(4)]
    stgb = [cx.sb("stgAb", [128, 512], BF16) for _ in range(4)]
    cnt = {"w": 0, "p": 0, "s": 0}

    def load_w(col0, ncols):
        wb = wbuf[cnt["w"] % 2]
        cnt["w"] += 1
        src = w_in.view(l * D * IN_WIDTH + col0, [[IN_WIDTH, 128], [128 * IN_WIDTH, 16], [1, ncols]])
        P.dma("gpsimd", lambda e: e.dma_start(out=wb.t[:, :, 0:ncols], in_=src), wb.b, reads=[w_in.b], writes=[wb.b])
        return wb

    def fm_group(col0, ncols, sub, evac):
        wb = load_w(col0, ncols)
        for j in range(ncols // sub):
            for blk in range(NB):
                ps = psum[cnt["p"] % 4]
                cnt["p"] += 1
                for kc in range(16):
                    P.op("tensor", lambda e, kc=kc, j=j, blk=blk, ps=ps: e.matmul(
                        ps.t[0:sub, :], lhsT=wb.t[:, kc, j * sub:(j + 1) * sub], rhs=hT[blk].t[:, kc, :],
                        start=(kc == 0), stop=(kc == 15)), reads=[wb.b, hT[blk].b], writes=[ps.b])
                evac(ps, j, blk)

    def tm_group(col0, ncols, evac):
        wb = load_w(col0, ncols)
        for t in range(NT):
            ps = psum[cnt["p"] % 4]
            cnt["p"] += 1
            for kc in range(16):
                P.op("tensor", lambda e, kc=kc, t=t, ps=ps: e.matmul(
                    ps.t[:, 0:ncols], lhsT=hT[t // 4].t[:, kc, (t % 4) * 128:(t % 4 + 1) * 128], rhs=wb.t[:, kc, 0:ncols],
                    start=(kc == 0), stop=(kc == 15)), reads=[wb.b, hT[t // 4].b], writes=[ps.b])
            evac(ps, t)

    def nxt(lst):
        s = lst[cnt["s"] % 4]
        cnt["s"] += 1
        return s

    qT = dr["qT"]
    for c0 in (0, 512):
        def ev_q(ps, j, blk, c0=c0):
            s = nxt(stgb)
            h = c0 // 64 + j
            P.op("scalar", lambda e: e.mul(out=s.t[0:64, :], in_=ps.t[0:64, :], mul=0.125), reads=[ps.b], writes=[s.b])
            P.dma("sync", lambda e: e.dma_start(out=qT.ap[h, :, blk * 512:(blk + 1) * 512], in_=s.t[0:64, :]), s.b,
                  reads=[s.b], writes=[qT.b])
        fm_group(ZQ + c0, 512, 64, ev_q)
    kT, vcT = dr["kT"], dr["vcT"]
    for kvi, dst, di in ((0, kT, 0), (1, vcT, None), (2, kT, 1), (4, kT, 2)):
        def ev_k(ps, j, blk, dst=dst, di=di):
            s = nxt(stgb)
            P.op("vector", lambda e: e.tensor_copy(out=s.t[0:64, :], in_=ps.t[0:64, :]), reads=[ps.b], writes=[s.b])
            o = dst.ap[di, j, :, blk * 512:(blk + 1) * 512] if di is not None else dst.ap[j, :, blk * 512:(blk + 1) * 512]
            P.dma("sync", lambda e: e.dma_start(out=o, in_=s.t[0:64, :]), s.b, reads=[s.b], writes=[dst.b])
        fm_group(ZKV + kvi * 256, 256, 64, ev_k)
    v = dr["v"]
    for kvi, di in ((3, 0), (5, 1)):
        def ev_v(ps, t, di=di):
            s = nxt(stgb)
            P.op("vector", lambda e: e.tensor_copy(out=s.t[:, 0:256], in_=ps.t[:, 0:256]), reads=[ps.b], writes=[s.b])
            P.dma("sync", lambda e: e.dma_start(out=v.ap[di, t * 128:(t + 1) * 128, :], in_=s.t[:, 0:256]), s.b,
                  reads=[s.b], writes=[v.b])
        tm_group(ZKV + kvi * 256, 256, ev_v)
    gl = dr["gl"]

    def ev_gl(ps, t):
        s = nxt(stg)
        P.op("vector", lambda e: e.tensor_copy(out=s.t[:, 0:48], in_=ps.t[:, 0:48]), reads=[ps.b], writes=[s.b])
        P.dma("sync", lambda e: e.dma_start(out=gl.ap[t * 128:(t + 1) * 128, :], in_=s.t[:, 0:48]), s.b, reads=[s.b], writes=[gl.b])
    tm_group(ZGL, 48, ev_gl)
    u = dr["u"]
    for c0 in (0, 512):
        def ev_u(ps, t, c0=c0):
            s = nxt(stg)
            P.op("scalar", lambda e: e.copy(out=s.t[:], in_=ps.t[:]), reads=[ps.b], writes=[s.b])
            P.dma("sync", lambda e: e.dma_start(out=u.ap[t * 128:(t + 1) * 128, c0:c0 + 512], in_=s.t[:]), s.b, reads=[s.b], writes=[u.b])
        tm_group(ZU + c0, 512, ev_u)
    gT = dr["gT"]
    for c0 in range(0, 4096, 512):
        def ev_g(ps, j, blk, c0=c0):
            s = nxt(stg)
            P.op("scalar", lambda e: e.activation(out=s.t[:], in_=ps.t[:], func=AF.Sigmoid), reads=[ps.b], writes=[s.b])
            r0 = c0 + j * 128
            P.dma("sync", lambda e: e.dma_start(out=gT.ap[r0:r0 + 128, blk * 512:(blk + 1) * 512], in_=s.t[:]), s.b, reads=[s.b], writes=[gT.b])
        fm_group(ZML + c0, 512, 128, ev_g)
    cx.close()

import math


def rel_bucket_np(dist):
    n = np.maximum(dist, 0)
    nf = np.maximum(n, 1).astype(np.float32)
    large = 16 + (np.log(nf / np.float32(16)) / np.float32(math.log(2048 / 16)) * np.float32(16)).astype(np.int32)
    return np.where(n < 16, n, np.minimum(large, 31))


LEN_S, M_S = 3200, 3072
LEN_W, M_W = 1664, 1536
LEN_C = 5248


def onehot_table(dist, win=None):
    L = dist.shape[0]
    oh = np.zeros((33, L), np.float32)
    masked = dist < 0
    if win is not None:
        masked = masked | (dist >= win)
    b = rel_bucket_np(dist)
    ok = ~masked
    oh[b[ok], np.nonzero(ok)[0]] += 1.0
    oh[31, ok] -= 1.0
    oh[32, masked] = NEG
    return oh


def core_tables(c, NT):
    SEQ = NT * 8 * 128
    n_sel = SEQ // 64
    NNT = SEQ // 2048
    n_cmp = SEQ // 16 - 1
    t = {}
    xs = np.arange(LEN_S)
    t["oh_sel"] = onehot_table(xs + 128 * (c - 7) - 127)
    xw = np.arange(LEN_W)
    t["oh_win"] = onehot_table(xw + 128 * (c - 7) - 127, win=512)
    xc = np.arange(LEN_C)
    t["oh_cmp"] = onehot_table(xc + 128 * c - 31 - 2032)
    q = np.arange(128)[:, None]
    m = np.arange(2 * n_sel)[None, :]
    rel = m - n_sel - 2 * c
    cur = (q >= 64).astype(np.int64)
    keep = np.ones((128, 2 * n_sel), np.float32)
    add = np.zeros((128, 2 * n_sel), np.float32)
    fut = rel > cur
    keep[fut] = 0.0
    add[fut] = -1.0
    is_cur = rel == cur
    keep[np.broadcast_to(is_cur, keep.shape)] = 0.0
    add[np.broadcast_to(is_cur, keep.shape)] = 10001.0
    is_prev = rel == cur - 1
    keep[np.broadcast_to(is_prev, keep.shape)] = 0.0
    add[np.broadcast_to(is_prev, keep.shape)] = 10002.0
    t["keepS"] = keep
    t["addS"] = add
    return t


def shared_tables(NT):
    SEQ = NT * 8 * 128
    n_sel = SEQ // 64
    NNT = SEQ // 2048
    n_cmp = SEQ // 16 - 1
    t = {}
    t["ident"] = np.eye(128, dtype=np.float32)
    t["jrev"] = np.eye(128, dtype=np.float32)[::-1].copy()
    jj = np.arange(128)[:, None]
    mm = np.arange(8192)[None, :]
    t["wstrip"] = (mm // 64 == jj).astype(np.float32)
    sa = np.zeros((48, 48 * 64), np.float32)
    for s in range(48):
        sa[s, s * 64:(s + 1) * 64] = 1.0
    t["selall"] = sa
    n = np.arange(NNT * 128)[:, None]
    j = np.arange(n_sel)[None, :]
    lo = np.maximum(n * 16, j * 64)
    hi = np.minimum(n * 16 + 32, (j + 1) * 64)
    ov = np.maximum(hi - lo, 0).astype(np.float32) / 32
    ov[n_cmp:, :] = 0.0
    t["ovl"] = ov.reshape(NNT, 128, n_sel).transpose(1, 0, 2).copy()
    return t


def phase_B(nc, P, NT, l, dr):
    NTOK = NT * 128
    SEQ = NTOK * 8
    n_sel = SEQ // 64
    NSELP = ((n_sel + 127) // 128) * 128
    NCH = NSELP // 128
    NNT = SEQ // 2048
    n_cmp = SEQ // 16 - 1
    NCP = NNT * 128
    cx = Cx(nc, P)
    idf, idb = make_ident(cx, dr["ident"])

    def load_const(name, shape, dt, src_dt):
        tl = cx.sb(name, shape, dt)
        eng = "sync" if dt == F32 else "gpsimd"
        P.dma(eng, lambda e: e.dma_start(out=tl.t[:], in_=src_dt.ap), tl.b, reads=[src_dt.b], writes=[tl.b])
        return tl
    jb = load_const("jb", [128, 128], BF16, dr["jrev"])
    wst = load_const("wst", [128, 8192], BF16, dr["wstrip"])
    selall = load_const("selall", [48, 48 * 64], F32, dr["selall"])
    ovl = load_const("ovl", [128, NNT, n_sel], BF16, dr["ovl"])
    keepS = load_const("keepS", [128, 2 * n_sel], F32, dr["keepS"])
    addS = load_const("addS", [128, 2 * n_sel], F32, dr["addS"])
    ones65 = cx.sb("ones65", [65, 64], F32)
    P.op("vector", lambda e: e.memset(ones65.t[:], 1.0), writes=[ones65.b])
    kcTa = cx.sb("kcTa", [65, 4, NCP], BF16)
    vca = cx.sb("vca", [128, 4, NNT, 65], BF16)
    P.op("vector", lambda e: e.memset(kcTa.t[:], 0.0), writes=[kcTa.b])
    P.op("vector", lambda e: e.memset(kcTa.t[64:65, :, :], 1.0), writes=[kcTa.b])
    P.op("vector", lambda e: e.memset(vca.t[:], 0.0), writes=[vca.b])
    P.op("vector", lambda e: e.memset(vca.t[:, :, :, 64:65], 1.0), writes=[vca.b])
    sgT = cx.sb("sgT", [48, NTOK], F32)
    b31 = cx.sb("b31", [65, 16], F32)
    rb = dr["rel_bias"]
    P.dma("sync", lambda e: e.dma_start(out=b31.t[64:65, :], in_=rb.view(31, [[0, 1], [32, 16]]), allow_slow_non_contiguous=True),
          b31.b, reads=[rb.b], writes=[b31.b])

    c0 = Cx(nc, P)
    tab33 = c0.sb("tab33", [33, 16], F32)
    P.op("vector", lambda e: e.memset(tab33.t[32:33, :], 1.0), writes=[tab33.b])
    P.dma("sync", lambda e: e.dma_start(out=tab33.t[0:32, :], in_=rb.view(0, [[1, 32], [32, 16]]), allow_slow_non_contiguous=True),
          tab33.b, reads=[rb.b], writes=[tab33.b])
    psg = [c0.ps("psg", [16, 512], F32) for _ in range(2)]
    kk = 0
    for name, LEN in (("sel", LEN_S), ("win", LEN_W), ("cmp", LEN_C)):
        oh = c0.sb("oh_" + name, [33, LEN], F32)
        src = dr["oh_" + name]
        P.dma("sync", lambda e, oh=oh, src=src: e.dma_start(out=oh.t[:], in_=src.ap), oh.b, reads=[src.b], writes=[oh.b])
        gs = c0.sb("gs_" + name, [16, LEN], BF16)
        for x0 in range(0, LEN, 512):
            n = min(512, LEN - x0)
            ps = psg[kk % 2]
            kk += 1
            P.op("tensor", lambda e, ps=ps, oh=oh, x0=x0, n=n: e.matmul(ps.t[:, 0:n], lhsT=tab33.t[:], rhs=oh.t[:, x0:x0 + n], start=True, stop=True),
                 reads=[tab33.b, oh.b], writes=[ps.b])
            P.op("vector", lambda e, ps=ps, gs=gs, x0=x0, n=n: e.tensor_copy(out=gs.t[:, x0:x0 + n], in_=ps.t[:, 0:n]), reads=[ps.b], writes=[gs.b])
        gd = dr["gd_" + name]
        P.dma("sync", lambda e, gs=gs, gd=gd: e.dma_start(out=gd.ap, in_=gs.t[:]), gs.b, reads=[gs.b], writes=[gd.b])
    glt = [c0.sb("glt", [128, 48], F32) for _ in range(2)]
    pst = [c0.ps("pst", [48, 128], F32) for _ in range(2)]
    gl = dr["gl"]
    for t in range(NT):
        g_ = glt[t % 2]
        ps = pst[t % 2]
        P.dma("sync", lambda e, t=t, g_=g_: e.dma_start(out=g_.t[:], in_=gl.ap[t * 128:(t + 1) * 128, :]), g_.b, reads=[gl.b], writes=[g_.b])
        P.op("scalar", lambda e, g_=g_: e.activation(out=g_.t[:], in_=g_.t[:], func=AF.Sigmoid), reads=[g_.b], writes=[g_.b])
        P.op("tensor", lambda e, g_=g_, ps=ps: e.transpose(out=ps.t[:], in_=g_.t[:], identity=idf.t[:]), reads=[g_.b, idf.b], writes=[ps.b])
        P.op("vector", lambda e, t=t, ps=ps: e.tensor_copy(out=sgT.t[:, t * 128:(t + 1) * 128], in_=ps.t[:]), reads=[ps.b], writes=[sgT.b])
    c0.close()

    c1 = Cx(nc, P)
    kTg, vcTg = dr["kTg"], dr["vcTg"]
    raw2 = [c1.sb("raw2", [128, SEQ + 1040], BF16) for _ in range(2)]
    w1b = [c1.sb("w1b", [128, 16, 128], BF16) for _ in range(2)]
    w2b = [c1.sb("w2b", [128, 64], BF16) for _ in range(2)]
    pe2 = [c1.sb("pe2", [128, 16], BF16) for _ in range(2)]
    biasv = [c1.sb("biasv", [128, 1], F32) for _ in range(2)]
    psb = c1.ps("psb", [128, 1], F32)
    psh = [c1.ps("psh", [128, 512], F32) for _ in range(2)]
    pso = [c1.ps("pso", [128, 512], F32) for _ in range(2)]
    xh = c1.sb("xh", [128, 512], F32)
    tq = c1.sb("tq", [128, 512], F32)
    sg = c1.sb("sg", [128, 512], F32)
    gel = [c1.sb("gel", [128, 512], BF16) for _ in range(2)]
    w1d, w2d, ped = dr["cmp_w1"], dr["cmp_w2"], dr["cmp_pe"]
    for r2 in raw2:
        P.op("vector", lambda e, r2=r2: e.memset(r2.t[:], 0.0), writes=[r2.b])
    kh = 0
    for kvi in range(2):
        P.dma("gpsimd", lambda e, kvi=kvi: e.dma_start(out=w1b[kvi].t[:], in_=w1d.view((l * 2 + kvi) * 2048 * 128, [[128, 128], [128 * 128, 16], [1, 128]])),
              w1b[kvi].b, reads=[w1d.b], writes=[w1b[kvi].b])
        P.dma("gpsimd", lambda e, kvi=kvi: e.dma_start(out=w2b[kvi].t[:], in_=w2d.ap[l, kvi, :, :]), w2b[kvi].b, reads=[w2d.b], writes=[w2b[kvi].b])
        P.dma("gpsimd", lambda e, kvi=kvi: e.dma_start(out=pe2[kvi].t[:], in_=ped.view((l * 2 + kvi) * 2048, [[1, 128], [128, 16]]), allow_slow_non_contiguous=True),
              pe2[kvi].b, reads=[ped.b], writes=[pe2[kvi].b])
        for c in range(16):
            P.op("tensor", lambda e, kvi=kvi, c=c: e.matmul(psb.t[:], lhsT=w1b[kvi].t[:, c, :], rhs=pe2[kvi].t[:, c:c + 1], start=(c == 0), stop=(c == 15)),
                 reads=[w1b[kvi].b, pe2[kvi].b], writes=[psb.b])
        P.op("vector", lambda e, kvi=kvi: e.tensor_copy(out=biasv[kvi].t[:], in_=psb.t[:]), reads=[psb.b], writes=[biasv[kvi].b])
        for g in range(4):
            r2 = raw2[(kvi * 4 + g) % 2]
            for r in range(8):
                if kvi == 0:
                    src = kTg.ap[r, 0, g, :, :]
                else:
                    src = vcTg.ap[r, g, :, :]
                src = src.rearrange("d (i p) -> d i p", p=128)
                for half, off in ((0, 1), (1, 0)):
                    base = off + r * 128
                    P.dma("sync", lambda e, r2=r2, src=src, half=half, base=base: e.dma_start(
                        out=r2.t[half * 64:(half + 1) * 64, base:base + NT * 1024].rearrange("d (i x) -> d i x", x=1024)[:, :, 0:128], in_=src),
                        r2.b, reads=[kTg.b if kvi == 0 else vcTg.b], writes=[r2.b])
            for n0 in range(0, n_cmp, 512):
                nn = min(512, n_cmp - n0)
                ph = psh[kh % 2]
                gl_ = gel[kh % 2]
                po = pso[kh % 2]
                kh += 1
                for c in range(16):
                    s0 = 1 + 16 * n0 + 2 * c
                    P.op("tensor", lambda e, c=c, ph=ph, r2=r2, s0=s0, nn=nn, kvi=kvi: e.matmul(
                        ph.t[:, 0:nn], lhsT=w1b[kvi].t[:, c, :], rhs=r2.t[:, s0:s0 + 16 * (nn - 1) + 1:16], start=(c == 0), stop=(c == 15)),
                        reads=[w1b[kvi].b, r2.b], writes=[ph.b])
                P.op("scalar", lambda e, ph=ph, nn=nn, kvi=kvi: e.activation(out=xh.t[:, 0:nn], in_=ph.t[:, 0:nn], func=AF.Identity, bias=biasv[kvi].t[:, 0:1]),
                     reads=[ph.b, biasv[kvi].b], writes=[xh.b])
                P.op("vector", lambda e, nn=nn: e.tensor_tensor(out=tq.t[:, 0:nn], in0=xh.t[:, 0:nn], in1=xh.t[:, 0:nn], op=ALU.mult), reads=[xh.b], writes=[tq.b])
                P.op("vector", lambda e, nn=nn: e.tensor_scalar(out=tq.t[:, 0:nn], in0=tq.t[:, 0:nn], scalar1=0.044715, scalar2=1.0, op0=ALU.mult, op1=ALU.add),
                     reads=[tq.b], writes=[tq.b])
                P.op("vector", lambda e, nn=nn: e.tensor_tensor(out=tq.t[:, 0:nn], in0=tq.t[:, 0:nn], in1=xh.t[:, 0:nn], op=ALU.mult), reads=[tq.b, xh.b], writes=[tq.b])
                P.op("scalar", lambda e, nn=nn: e.activation(out=sg.t[:, 0:nn], in_=tq.t[:, 0:nn], func=AF.Sigmoid, scale=1.5957691216057308),
                     reads=[tq.b], writes=[sg.b])
                P.op("vector", lambda e, nn=nn, gl_=gl_: e.tensor_tensor(out=gl_.t[:, 0:nn], in0=xh.t[:, 0:nn], in1=sg.t[:, 0:nn], op=ALU.mult),
                     reads=[xh.b, sg.b], writes=[gl_.b])
                if kvi == 0:
                    P.op("tensor", lambda e, po=po, gl_=gl_, nn=nn: e.matmul(po.t[0:64, 0:nn], lhsT=w2b[0].t[:], rhs=gl_.t[:, 0:nn], start=True, stop=True),
                         reads=[w2b[0].b, gl_.b], writes=[po.b])
                    P.op("scalar", lambda e, po=po, g=g, n0=n0, nn=nn: e.copy(out=kcTa.t[0:64, g, n0:n0 + nn], in_=po.t[0:64, 0:nn]), reads=[po.b], writes=[kcTa.b])
                else:
                    for s in range(0, nn, 128):
                        ns = min(128, nn - s)
                        P.op("tensor", lambda e, po=po, gl_=gl_, s=s, ns=ns: e.matmul(po.t[0:ns, s // 128 * 64:s // 128 * 64 + 64], lhsT=gl_.t[:, s:s + ns], rhs=w2b[1].t[:],
                                                                               start=True, stop=True), reads=[w2b[1].b, gl_.b], writes=[po.b])
                    for s in range(0, nn, 128):
                        ns = min(128, nn - s)
                        nt = (n0 + s) // 128
                        P.op("scalar", lambda e, po=po, g=g, s=s, ns=ns, nt=nt: e.copy(out=vca.t[0:ns, g, nt, 0:64], in_=po.t[0:ns, s // 128 * 64:s // 128 * 64 + 64]),
                             reads=[po.b], writes=[vca.b])
    c1.close()
    return cx, dict(idf=idf, idb=idb, jb=jb, wst=wst, selall=selall, ovl=ovl, keepS=keepS, addS=addS, ones65=ones65,
                    kcTa=kcTa, vca=vca, sgT=sgT, b31=b31)


def phase_B2(nc, P, NT, l, dr, cx, env):
    NTOK = NT * 128
    SEQ = NTOK * 8
    n_sel = SEQ // 64
    NSELP = ((n_sel + 127) // 128) * 128
    NCH = NSELP // 128
    NNT = SEQ // 2048
    idf, idb, jb, wst, selall, ovl = env["idf"], env["idb"], env["jb"], env["wst"], env["selall"], env["ovl"]
    keepS, addS, ones65, kcTa, vca, sgT, b31 = env["keepS"], env["addS"], env["ones65"], env["kcTa"], env["vca"], env["sgT"], env["b31"]
    kTg, vg, qT, aT = dr["kTg"], dr["vg"], dr["qT"], dr["aT"]
    qta = cx.sb("qta", [65, 4, NTOK], BF16)
    kTs = cx.sb("kTs", [65, 8, NTOK], BF16)
    vs = cx.sb("vs", [128, 8, NT, 65], BF16)
    Fs = cx.sb("Fs", [128, 4, M_S], BF16)
    Fw = cx.sb("Fw", [128, 4, M_W], BF16)
    Fc = cx.sb("Fc", [128, 4, 4, 128], BF16)
    P.op("vector", lambda e: e.memset(kTs.t[64:65, :, :], 1.0), writes=[kTs.b])
    P.op("vector", lambda e: e.memset(vs.t[:, :, :, 64:65], 1.0), writes=[vs.b])
    kTw = [cx.sb("kTw", [65, 12, 128], BF16) for _ in range(2)]
    vw = [cx.sb("vw", [128, 12, 65], BF16) for _ in range(2)]
    for b_ in kTw:
        P.op("vector", lambda e, b_=b_: e.memset(b_.t[64:65, :, :], 1.0), writes=[b_.b])
    for b_ in vw:
        P.op("vector", lambda e, b_=b_: e.memset(b_.t[:, :, 64:65], 1.0), writes=[b_.b])
    psR = [cx.ps("psR", [128, 512], F32) for _ in range(3)]
    psOs = cx.ps("psOs", [65, 512], F32)
    psOw = cx.ps("psOw", [65, 512], F32)
    psOc = cx.ps("psOc", [65, 512], F32)
    psU = [cx.ps("psU", [128, 2, 256], F32) for _ in range(2)]
    Eb = [cx.sb("Eb", [128, 512], BF16) for _ in range(4)]
    Osb = [cx.sb("Osb", [65, 512], F32) for _ in range(2)]
    rrec = cx.sb("rrec", [64, 512], F32)
    fac = cx.sb("fac", [64, 512], F32)
    acc = cx.sb("acc", [64, 512], F32)
    prod = cx.sb("prod", [64, 512], F32)
    aTs = [cx.sb("aTs", [64, 4, 128], BF16) for _ in range(2)]
    rs = cx.sb("rs", [128, 4], F32)
    imp = cx.sb("imp", [128, n_sel], F32)
    rep = cx.sb("rep", [128, n_sel], F32)
    m8 = cx.sb("m8", [128, 16], F32)
    selm = cx.sb("selm", [128, n_sel], F32)
    mb = cx.sb("mb", [128, NSELP], BF16)
    P.op("vector", lambda e: e.memset(mb.t[:], 0.0), writes=[mb.b])
    mbT = [cx.sb("mbT", [128, NCH, 4, 128], BF16) for _ in range(2)]
    st = {"r": 0, "e": 0, "w": 0, "u": 0}
    gd_sel, gd_win, gd_cmp = dr["gd_sel"], dr["gd_win"], dr["gd_cmp"]

    def nextR():
        p = psR[st["r"] % 3]
        st["r"] += 1
        return p

    def nextE():
        p = Eb[st["e"] % 4]
        st["e"] += 1
        return p

    for g in range(4):
        P.dma("sync", lambda e, g=g: e.dma_start(out=qta.t[0:64, :, :], in_=qT.ap[4 * g:4 * g + 4, :, :].rearrange("h d t -> d h t")),
              qta.b, reads=[qT.b], writes=[qta.b])
        P.op("vector", lambda e, g=g: e.tensor_copy(out=qta.t[64:65, :, :], in_=b31.t[64:65, 4 * g:4 * g + 4].unsqueeze(2).to_broadcast([1, 4, NTOK])),
             reads=[b31.b], writes=[qta.b])
        for r in range(8):
            P.dma("sync", lambda e, g=g, r=r: e.dma_start(out=kTs.t[0:64, r, :], in_=kTg.ap[r, 1, g, :, :]), kTs.b, reads=[kTg.b], writes=[kTs.b])
            P.dma("sync", lambda e, g=g, r=r: e.dma_start(out=vs.t[:, r, :, 0:64], in_=vg.ap[r, 0, :, g * 64:(g + 1) * 64].rearrange("(i p) d -> p i d", p=128)),
                  vs.b, reads=[vg.b], writes=[vs.b])
        P.dma("sync", lambda e, g=g: e.dma_start(out=Fs.t[:], in_=gd_sel.view(4 * g * LEN_S, [[1, 128], [LEN_S, 4], [1, M_S]])), Fs.b, reads=[gd_sel.b], writes=[Fs.b])
        P.dma("sync", lambda e, g=g: e.dma_start(out=Fw.t[:], in_=gd_win.view(4 * g * LEN_W, [[1, 128], [LEN_W, 4], [1, M_W]])), Fw.b, reads=[gd_win.b], writes=[Fw.b])
        for b_ in range(2):
            for e_ in range(2):
                P.dma("sync", lambda e, g=g, b_=b_, e_=e_: e.dma_start(out=Fc.t[:, b_ * 2 + e_, :, :],
                      in_=gd_cmp.view(4 * g * LEN_C + 1024 * b_ + 2048 * e_, [[16, 128], [LEN_C, 4], [1, 128]])), Fc.b, reads=[gd_cmp.b], writes=[Fc.b])
        for i in range(NT):
            u_ = st["u"]
            st["u"] += 1
            qsl = qta.t[:, :, i * 128:(i + 1) * 128]
            kw, vw_ = kTw[u_ % 2], vw[u_ % 2]
            wt = []
            for a, jr in ((0, range(8)), (1, range(4, 8))):
                ii = i - a
                if ii < 0:
                    continue
                for j in jr:
                    slot = j if a == 0 else 4 + j
                    wt.append((slot, a, j))
                j0, nj = (0, 8) if a == 0 else (4, 4)
                s0 = 0 if a == 0 else 8
                P.dma("sync", lambda e, g=g, ii=ii, j0=j0, nj=nj, s0=s0, kw=kw: e.dma_start(
                    out=kw.t[0:64, s0:s0 + nj, :], in_=kTg.ap[j0:j0 + nj, 2, g, :, ii * 128:(ii + 1) * 128].rearrange("r d p -> d r p")),
                    kw.b, reads=[kTg.b], writes=[kw.b])
                P.dma("sync", lambda e, g=g, ii=ii, j0=j0, nj=nj, s0=s0, vw_=vw_: e.dma_start(
                    out=vw_.t[:, s0:s0 + nj, 0:64], in_=vg.ap[j0:j0 + nj, 1, ii * 128:(ii + 1) * 128, g * 64:(g + 1) * 64].rearrange("r p d -> p r d")),
                    vw_.b, reads=[vg.b], writes=[vw_.b])
            nts = list(range(i // 2 + 1))
            ntl = nts[-1]
            for nt in nts:
                e_ = i // 2 - nt
                near = e_ <= 1
                S = nextR()
                P.op("tensor", lambda e, S=S, nt=nt, qsl=qsl, near=near, g=g: e.matmul(S.t[:], lhsT=kcTa.t[:, g, nt * 128:(nt + 1) * 128], rhs=qsl, start=True, stop=not near),
                     reads=[kcTa.b, qta.b], writes=[S.b])
                if near:
                    cmb = (i % 2) * 2 + e_
                    P.op("tensor", lambda e, S=S, cmb=cmb: e.matmul(S.t[:], lhsT=jb.t[:], rhs=Fc.t[:, cmb, :, :], start=False, stop=True),
                         reads=[jb.b, Fc.b], writes=[S.b])
                E = nextE()
                P.op("scalar", lambda e, S=S, E=E: e.activation(out=E.t[:], in_=S.t[:], func=AF.Exp), reads=[S.b], writes=[E.b])
                P.op("tensor", lambda e, E=E, nt=nt, g=g, ntl=ntl: e.matmul(psOc.t[:], lhsT=vca.t[:, g, nt, :], rhs=E.t[:], start=(nt == 0), stop=(nt == ntl)),
                     reads=[vca.b, E.b], writes=[psOc.b])
                for h in range(4):
                    P.op("tensor", lambda e, E=E, nt=nt, h=h, ntl=ntl: e.matmul(psU[h // 2].t[:, h % 2, 0:n_sel], lhsT=E.t[:, h * 128:(h + 1) * 128], rhs=ovl.t[:, nt, :],
                                                                    start=(nt == 0 and h % 2 == 0), stop=(nt == ntl), skip_group_check=True), reads=[ovl.b, E.b], writes=[psU[h // 2].b])
            def attend(tiles, psO, kfn, vfn, mask_fn, bias_fn, qsl=qsl, kw=kw, vw_=vw_):
                nt_ = len(tiles)
                pend = []

                def pv(item):
                    E, va, idx = item
                    P.op("tensor", lambda e, E=E, va=va, idx=idx, nt_=nt_, psO=psO: e.matmul(psO.t[:], lhsT=va, rhs=E.t[:], start=(idx == 0), stop=(idx == nt_ - 1)),
                         reads=[vs.b, vw_.b, E.b], writes=[psO.b])
                for idx, tl in enumerate(tiles):
                    S = nextR()
                    mk = mask_fn(tl)
                    bs = bias_fn(tl)
                    ka = kfn(tl)
                    va = vfn(tl)
                    P.op("tensor", lambda e, S=S, ka=ka, mk=mk, bs=bs, qsl=qsl: e.matmul(S.t[:], lhsT=ka, rhs=qsl, start=True, stop=(mk is None and bs is None)),
                         reads=[kTs.b, kw.b, qta.b], writes=[S.b])
                    if mk is not None:
                        P.op("tensor", lambda e, S=S, mk=mk, bs=bs: e.matmul(S.t[:], lhsT=mk[0], rhs=mk[1], start=False, stop=(bs is None)),
                             reads=[wst.b, mk[2]], writes=[S.b])
                    if bs is not None:
                        P.op("tensor", lambda e, S=S, bs=bs: e.matmul(S.t[:], lhsT=jb.t[:], rhs=bs[0], start=False, stop=True),
                             reads=[jb.b, bs[1]], writes=[S.b])
                    E = nextE()
                    P.op("scalar", lambda e, S=S, E=E: e.activation(out=E.t[:], in_=S.t[:], func=AF.Exp), reads=[S.b], writes=[E.b])
                    pend.append((E, va, idx))
                    if len(pend) > 2:
                        pv(pend.pop(0))
                while pend:
                    pv(pend.pop(0))
            attend(wt, psOw,
                   lambda tl, kw=kw: kw.t[:, tl[0], :],
                   lambda tl, vw_=vw_: vw_.t[:, tl[0], :],
                   lambda tl: None,
                   lambda tl: (Fw.t[:, :, 128 * (8 * tl[1] + 7 - tl[2]):128 * (8 * tl[1] + 7 - tl[2]) + 128], Fw.b))
            for hh in range(2):
                P.op("vector", lambda e, hh=hh: e.tensor_reduce(out=rs.t[:, 2 * hh:2 * hh + 2], in_=psU[hh].t[:, :, 0:n_sel], axis=AX.X, op=ALU.add), reads=[psU[hh].b], writes=[rs.b])
            P.op("vector", lambda e: e.tensor_scalar(out=rs.t[:], in0=rs.t[:], scalar1=1e-30, scalar2=None, op0=ALU.max), reads=[rs.b], writes=[rs.b])
            P.op("vector", lambda e: e.reciprocal(out=rs.t[:], in_=rs.t[:]), reads=[rs.b], writes=[rs.b])
            P.op("vector", lambda e: e.tensor_scalar(out=imp.t[:], in0=psU[0].t[:, 0, 0:n_sel], scalar1=rs.t[:, 0:1], scalar2=None, op0=ALU.mult),
                 reads=[psU[0].b, rs.b], writes=[imp.b])
            for h in range(1, 4):
                P.op("vector", lambda e, h=h: e.scalar_tensor_tensor(out=imp.t[:], in0=psU[h // 2].t[:, h % 2, 0:n_sel], scalar=rs.t[:, h:h + 1], in1=imp.t[:], op0=ALU.mult, op1=ALU.add),
                     reads=[psU[h // 2].b, rs.b, imp.b], writes=[imp.b])
            so = n_sel - 16 * i
            P.op("vector", lambda e, so=so: e.tensor_tensor(out=imp.t[:], in0=imp.t[:], in1=keepS.t[:, so:so + n_sel], op=ALU.mult), reads=[imp.b, keepS.b], writes=[imp.b])
            P.op("vector", lambda e, so=so: e.tensor_tensor(out=imp.t[:], in0=imp.t[:], in1=addS.t[:, so:so + n_sel], op=ALU.add), reads=[imp.b, addS.b], writes=[imp.b])
            P.op("vector", lambda e: e.memset(imp.t[:, 0:1], 10000.0), reads=[imp.b], writes=[imp.b])
            P.op("vector", lambda e: e.max(out=m8.t[:, 0:8], in_=imp.t[:]), reads=[imp.b], writes=[m8.b])
            P.op("vector", lambda e: e.match_replace(out=rep.t[:], in_to_replace=m8.t[:, 0:8], in_values=imp.t[:], imm_value=-5.0), reads=[imp.b, m8.b], writes=[rep.b])
            P.op("vector", lambda e: e.max(out=m8.t[:, 8:16], in_=rep.t[:]), reads=[rep.b, m8.b], writes=[m8.b])
            P.op("vector", lambda e: e.tensor_scalar(out=selm.t[:], in0=imp.t[:], scalar1=m8.t[:, 15:16], scalar2=None, op0=ALU.is_ge), reads=[imp.b, m8.b], writes=[selm.b])
            P.op("vector", lambda e: e.scalar_tensor_tensor(out=selm.t[:], in0=imp.t[:], scalar=0.0, in1=selm.t[:], op0=ALU.is_ge, op1=ALU.mult),
                 reads=[imp.b, selm.b], writes=[selm.b])
            P.op("vector", lambda e: e.tensor_scalar(out=mb.t[:, 0:n_sel], in0=selm.t[:], scalar1=-NEG, scalar2=NEG, op0=ALU.mult, op1=ALU.add),
                 reads=[selm.b], writes=[mb.b])
            if "dbg" in dr and g == 0:
                dbg = dr["dbg"]
                P.dma("sync", lambda e, i=i: e.dma_start(out=dbg.ap[i, :, 0:n_sel], in_=imp.t[:]), imp.b, reads=[imp.b], writes=[dbg.b])
                P.dma("sync", lambda e, i=i: e.dma_start(out=dbg.ap[i, :, 256:256 + n_sel], in_=selm.t[:]), selm.b, reads=[selm.b], writes=[dbg.b])
                P.dma("sync", lambda e, i=i: e.dma_start(out=dbg.ap[i, :, 512:528], in_=m8.t[:]), m8.b, reads=[m8.b], writes=[dbg.b])
            mT = mbT[u_ % 2]
            for ch in range(NCH):
                pt = nextR()
                ptv = pt.t[:].bitcast(BF16)[:, 0:128]
                P.op("tensor", lambda e, ch=ch, ptv=ptv: e.transpose(out=ptv, in_=mb.t[:, ch * 128:(ch + 1) * 128], identity=idb.t[:]), reads=[mb.b, idb.b], writes=[pt.b])
                P.op("vector", lambda e, ch=ch, ptv=ptv, mT=mT: e.tensor_copy(out=mT.t[:, ch, :, :], in_=ptv.unsqueeze(1).to_broadcast([128, 4, 128])),
                     reads=[pt.b], writes=[mT.b])
            stl = [(ip, j) for ip in range(i + 1) for j in range(8)]
            attend(stl, psOs,
                   lambda tl: kTs.t[:, tl[1], tl[0] * 128:(tl[0] + 1) * 128],
                   lambda tl: vs.t[:, tl[1], tl[0], :],
                   lambda tl: (wst.t[:, 128 * ((8 * tl[0] + tl[1]) % 64):128 * ((8 * tl[0] + tl[1]) % 64) + 128], mT.t[:, (8 * tl[0] + tl[1]) // 64, :, :], mT.b),
                   lambda tl: ((Fs.t[:, :, 128 * (8 * (i - tl[0]) + 7 - tl[1]):128 * (8 * (i - tl[0]) + 7 - tl[1]) + 128], Fs.b) if i - tl[0] <= 2 else None))
            for bi, psO in enumerate((psOc, psOs, psOw)):
                osb = Osb[bi % 2]
                P.op("scalar", lambda e, osb=osb, psO=psO: e.copy(out=osb.t[:], in_=psO.t[:]), reads=[psO.b], writes=[osb.b])
                pd = nextR()
                P.op("tensor", lambda e, pd=pd, osb=osb: e.matmul(pd.t[0:64, :], lhsT=ones65.t[64:65, :], rhs=osb.t[64:65, :], start=True, stop=True),
                     reads=[ones65.b, osb.b], writes=[pd.b])
                P.op("vector", lambda e, pd=pd: e.tensor_scalar(out=rrec.t[:], in0=pd.t[0:64, :], scalar1=1e-30, scalar2=None, op0=ALU.max), reads=[pd.b], writes=[rrec.b])
                P.op("vector", lambda e: e.reciprocal(out=rrec.t[:], in_=rrec.t[:]), reads=[rrec.b], writes=[rrec.b])
                pg = nextR()
                for h in range(4):
                    srow = 3 * (4 * g + h) + bi
                    P.op("tensor", lambda e, pg=pg, h=h, srow=srow, i=i: e.matmul(pg.t[0:64, h * 128:(h + 1) * 128], lhsT=selall.t[:, srow * 64:(srow + 1) * 64],
                                                                           rhs=sgT.t[:, i * 128:(i + 1) * 128], start=True, stop=True), reads=[selall.b, sgT.b], writes=[pg.b])
                P.op("vector", lambda e, pg=pg: e.tensor_tensor(out=fac.t[:], in0=pg.t[0:64, :], in1=rrec.t[:], op=ALU.mult), reads=[pg.b, rrec.b], writes=[fac.b])
                if bi == 0:
                    P.op("vector", lambda e, osb=osb: e.tensor_tensor(out=acc.t[:], in0=osb.t[0:64, :], in1=fac.t[:], op=ALU.mult), reads=[osb.b, fac.b], writes=[acc.b])
                else:
                    P.op("gpsimd", lambda e, osb=osb: e.tensor_tensor(out=prod.t[:], in0=osb.t[0:64, :], in1=fac.t[:], op=ALU.mult), reads=[osb.b, fac.b], writes=[prod.b])
                    P.op("gpsimd", lambda e: e.tensor_tensor(out=acc.t[:], in0=acc.t[:], in1=prod.t[:], op=ALU.add), reads=[acc.b, prod.b], writes=[acc.b])
            ao = aTs[u_ % 2]
            P.op("vector", lambda e, ao=ao: e.tensor_copy(out=ao.t[:], in_=acc.t[:].rearrange("d (h q) -> d h q", h=4)), reads=[acc.b], writes=[ao.b])
            P.dma("sync", lambda e, ao=ao, g=g, i=i: e.dma_start(out=aT.ap[4 * g:4 * g + 4, :, i * 128:(i + 1) * 128].rearrange("h d t -> d h t"), in_=ao.t[:]),
                  ao.b, reads=[ao.b], writes=[aT.b])
    cx.close()

D_FF = 5632
C_STOP = 0
NFF = D_FF // 128


def pool_tables(c):
    wins = (2, 4, 8, 16)
    main = np.zeros((2, 128, 4, 128), np.float32)
    hA = np.zeros((128, 4, 128), np.float32)
    hB = np.zeros((16, 4, 128), np.float32)
    s = np.arange(128)[:, None]
    t = np.arange(128)[None, :]
    for gi, w in enumerate(wins):
        band = ((s <= t) & (s >= t - w + 1)).astype(np.float32)
        main[1, :, gi, :] = band / w - (s == t)
        if c == 0:
            cnt = np.minimum(t + 1, w).astype(np.float32)
            main[0, :, gi, :] = band / cnt - (s == t)
        else:
            main[0, :, gi, :] = main[1, :, gi, :]
        hs = np.arange(16)[:, None] - 16
        hb = ((hs >= t - w + 1)).astype(np.float32) / w
        if c >= 1:
            hA[(c - 1) * 16:(c - 1) * 16 + 16, gi, :] = hb
        else:
            hB[:, gi, :] = hb
    return {"poolM": main, "poolHA": hA, "poolHB": hB}


def post_norm_residual(cx, ysb, xt_ap, xb, gbc, scr, eps=1e-6):
    P = cx.P
    junk, ssq, rstd, tmp = scr
    P.op("scalar", lambda e: e.activation(out=junk.t[:], in_=ysb.t[:], func=AF.Square, accum_out=ssq.t[:]), reads=[ysb.b], writes=[junk.b, ssq.b])
    P.op("scalar", lambda e: e.activation(out=rstd.t[:], in_=ssq.t[:], func=AF.Sqrt, scale=1.0 / D, bias=eps), reads=[ssq.b], writes=[rstd.b])
    P.op("vector", lambda e: e.reciprocal(out=rstd.t[:], in_=rstd.t[:]), reads=[rstd.b], writes=[rstd.b])
    P.op("vector", lambda e: e.scalar_tensor_tensor(out=tmp.t[:], in0=ysb.t[:], scalar=rstd.t[:, 0:1], in1=gbc.t[:], op0=ALU.mult, op1=ALU.mult),
         reads=[ysb.b, rstd.b, gbc.b], writes=[tmp.b])
    P.op("gpsimd", lambda e: e.tensor_tensor(out=xt_ap, in0=xt_ap, in1=tmp.t[:], op=ALU.add), reads=[xb.b, tmp.b], writes=[xb.b])


def phase_C(nc, P, NT, l, dr):
    NTOK = NT * 128
    NB = NT // 4
    x, xo = dr["x"], dr["x_out"]
    aT, u, uhg, gT, mem = dr["aT"], dr["u"], dr["uhg"], dr["gT"], dr["mem"]
    cxo = Cx(nc, P)
    idf, idb = make_ident(cxo, dr["ident"])
    ones_b = cxo.sb("ones_b", [128, 128], BF16)
    P.op("vector", lambda e: e.memset(ones_b.t[:], 1.0), writes=[ones_b.b])
    KmT = cxo.sb("KmT", [128, 4, 256], BF16)
    Vm = cxo.sb("Vm", [128, 2, 512], BF16)
    xb = cxo.sb("xb", [128, 4, D], F32)
    tmp = cxo.sb("tmp", [128, D], F32)
    ssq = cxo.sb("ssq", [128, 1], F32)
    rstd = cxo.sb("rstd", [128, 1], F32)
    scr = (tmp, ssq, rstd, tmp)
    ysb = [cxo.sb("ysb", [128, D], F32) for _ in range(4)]

    def wload(cx_, name, src_dt, base, row_stride, kc, ncols, dst=None):
        tl = dst if dst is not None else cx_.sb(name, [128, kc, ncols], BF16)
        src = src_dt.view(base, [[row_stride, 128], [128 * row_stride, kc], [1, ncols]])
        P.dma("gpsimd", lambda e: e.dma_start(out=tl.t[:, 0:kc, 0:ncols], in_=src), tl.b, reads=[src_dt.b], writes=[tl.b])
        return tl

    cm = Cx(nc, P)
    nrm = NormT(cm, idb)
    gmem = load_gain_fm(cm, dr["ln_mem"], l)
    memT = cm.sb("memT", [128, 16, 256], BF16)
    mx = [cm.sb("mx", [128, D], F32) for _ in range(2)]
    for mt in range(2):
        P.dma("sync", lambda e, mt=mt: e.dma_start(out=mx[mt].t[:], in_=mem.ap[mt * 128:(mt + 1) * 128, :]), mx[mt].b, reads=[mem.b], writes=[mx[mt].b])
        nrm.run(mx[mt], gmem, memT, mt * 128)
    psm = [cm.ps("psm", [128, 512], F32) for _ in range(2)]
    wk = wload(cm, "wk", dr["w_xkv"], l * D * 1024, 1024, 16, 512)
    wv = wload(cm, "wv", dr["w_xkv"], l * D * 1024 + 512, 1024, 16, 512)
    for h in range(4):
        ps = psm[h % 2]
        for kc in range(16):
            P.op("tensor", lambda e, ps=ps, h=h, kc=kc: e.matmul(ps.t[:, 0:256], lhsT=wk.t[:, kc, h * 128:(h + 1) * 128], rhs=memT.t[:, kc, :], start=(kc == 0), stop=(kc == 15)),
                 reads=[wk.b, memT.b], writes=[ps.b])
        P.op("vector", lambda e, ps=ps, h=h: e.tensor_copy(out=KmT.t[:, h, :], in_=ps.t[:, 0:256]), reads=[ps.b], writes=[KmT.b])
    for mt in range(2):
        ps = psm[mt % 2]
        for kc in range(16):
            P.op("tensor", lambda e, ps=ps, mt=mt, kc=kc: e.matmul(ps.t[:], lhsT=memT.t[:, kc, mt * 128:(mt + 1) * 128], rhs=wv.t[:, kc, :], start=(kc == 0), stop=(kc == 15)),
                 reads=[wv.b, memT.b], writes=[ps.b])
        P.op("vector", lambda e, ps=ps, mt=mt: e.tensor_copy(out=Vm.t[:, mt, :], in_=ps.t[:]), reads=[ps.b], writes=[Vm.b])
    cm.close()
    if C_STOP == 1:
        P.disabled = True

    for blk in range(NB):
        t0 = blk * 512
        for tt in range(4):
            P.dma("sync", lambda e, tt=tt, t0=t0: e.dma_start(out=xb.t[:, tt, :], in_=x.ap[t0 + tt * 128:t0 + (tt + 1) * 128, :]), xb.b, reads=[x.b], writes=[xb.b])
        c1 = Cx(nc, P)
        gbc = load_gain_bc(c1, dr["ln_mix_post"], l)
        pTb = c1.sb("pTb", [128, 8, 512], BF16)
        aTb = c1.sb("aTb", [128, 8, 512], BF16)
        y1T = c1.sb("y1T", [128, 16, 512], BF16)
        psc = [c1.ps("psc", [128, 512], F32) for _ in range(6)]
        c1a = Cx(nc, P)
        poolM = c1a.sb("poolM", [128, 2, 4, 128], F32)
        P.dma("sync", lambda e: e.dma_start(out=poolM.t[:], in_=dr["poolM"].ap.rearrange("f s w t -> s f w t")), poolM.b, reads=[dr["poolM"].b], writes=[poolM.b])
        poolHA = c1a.sb("poolHA", [128, 4, 128], F32)
        P.dma("sync", lambda e: e.dma_start(out=poolHA.t[:], in_=dr["poolHA"].ap), poolHA.b, reads=[dr["poolHA"].b], writes=[poolHA.b])
        poolHB = c1a.sb("poolHB", [16, 4, 128], F32)
        P.dma("sync", lambda e: e.dma_start(out=poolHB.t[:], in_=dr["poolHB"].ap), poolHB.b, reads=[dr["poolHB"].b], writes=[poolHB.b])
        ut = [c1a.sb("ut", [128, 1024], F32) for _ in range(2)]
        hA = [c1a.sb("hA", [128, 1024], F32) for _ in range(2)]
        hB = [c1a.sb("hB", [16, 1024], F32) for _ in range(1)]
        yT = c1a.sb("yT", [128, 8, 512], BF16)
        kp = [0]

        def nps():
            p = psc[kp[0] % 6]
            kp[0] += 1
            return p
        P.dma("sync", lambda e, t0=t0: e.dma_start(out=aTb.t[:], in_=aT.ap[:, :, t0:t0 + 512].rearrange("h d t -> (h d) t").rearrange("(k p) t -> p k t", p=128)),
              aTb.b, reads=[aT.b], writes=[aTb.b])
        for tt in range(4):
            i = blk * 4 + tt
            u_, ha, hb = ut[tt % 2], hA[tt % 2], hB[0]
            P.dma("sync", lambda e, i=i, u_=u_: e.dma_start(out=u_.t[:], in_=u.ap[i * 128:(i + 1) * 128, :]), u_.b, reads=[u.b], writes=[u_.b])
            for r in range(8):
                P.dma("sync", lambda e, i=i, r=r, ha=ha: e.dma_start(out=ha.t[r * 16:(r + 1) * 16, :], in_=uhg.ap[r, i, :, :]), ha.b, reads=[uhg.b], writes=[ha.b])
            if i >= 1:
                P.dma("sync", lambda e, i=i, hb=hb: e.dma_start(out=hb.t[:], in_=uhg.ap[7, i - 1, :, :]), hb.b, reads=[uhg.b], writes=[hb.b])
            f = 0 if i == 0 else 1
            ps = nps()
            ps2 = nps()
            for cc in range(8):
                gi = cc // 2
                pp = ps if cc < 4 else ps2
                o = pp.t[:, (cc % 4) * 128:(cc % 4 + 1) * 128]
                P.op("tensor", lambda e, o=o, cc=cc, gi=gi, u_=u_, f=f: e.matmul(o, lhsT=u_.t[:, cc * 128:(cc + 1) * 128], rhs=poolM.t[:, f, gi, :], start=True, stop=False),
                     reads=[u_.b, poolM.b], writes=[pp.b])
                P.op("tensor", lambda e, o=o, cc=cc, gi=gi, ha=ha, i=i: e.matmul(o, lhsT=ha.t[:, cc * 128:(cc + 1) * 128], rhs=poolHA.t[:, gi, :], start=False, stop=(i == 0)),
                     reads=[ha.b, poolHA.b], writes=[pp.b])
                if i >= 1:
                    P.op("tensor", lambda e, o=o, cc=cc, gi=gi, hb=hb: e.matmul(o, lhsT=hb.t[:, cc * 128:(cc + 1) * 128], rhs=poolHB.t[:, gi, :], start=False, stop=True),
                         reads=[hb.b, poolHB.b], writes=[pp.b])
            P.op("vector", lambda e, ps=ps, tt=tt: e.tensor_copy(out=yT.t[:, 0:4, tt * 128:(tt + 1) * 128], in_=ps.t[:].rearrange("p (c t) -> p c t", c=4)), reads=[ps.b], writes=[yT.b])
            P.op("scalar", lambda e, ps2=ps2, tt=tt: e.copy(out=yT.t[:, 4:8, tt * 128:(tt + 1) * 128], in_=ps2.t[:].rearrange("p (c t) -> p c t", c=4)), reads=[ps2.b], writes=[yT.b])
        wp = c1a.sb("wp", [128, 8, 256], BF16)
        P.dma("gpsimd", lambda e: e.dma_start(out=wp.t[:], in_=dr["w_pool"].view(l * 4 * 256 * 256, [[256, 128], [128 * 256, 8], [1, 256]])), wp.b, reads=[dr["w_pool"].b], writes=[wp.b])
        psc_fm = c1a.sb("pscale", [128, 8], F32)
        P.dma("sync", lambda e: e.dma_start(out=psc_fm.t[:], in_=dr["pool_scale"].view(l * 1024, [[1, 128], [128, 8]]), allow_slow_non_contiguous=True),
              psc_fm.b, reads=[dr["pool_scale"].b], writes=[psc_fm.b])
        for gi in range(4):
            for oc in range(2):
                ps = nps()
                for ci in range(2):
                    P.op("tensor", lambda e, ps=ps, gi=gi, oc=oc, ci=ci: e.matmul(ps.t[:], lhsT=wp.t[:, gi * 2 + ci, oc * 128:(oc + 1) * 128], rhs=yT.t[:, gi * 2 + ci, :],
                                                                              start=(ci == 0), stop=(ci == 1)), reads=[wp.b, yT.b], writes=[ps.b])
                P.op("vector", lambda e, ps=ps, gi=gi, oc=oc: e.tensor_scalar(out=pTb.t[:, gi * 2 + oc, :], in0=ps.t[:], scalar1=psc_fm.t[:, gi * 2 + oc:gi * 2 + oc + 1], scalar2=None, op0=ALU.mult),
                     reads=[ps.b, psc_fm.b], writes=[pTb.b])
        c1a.close()
        if C_STOP == 2:
            P.disabled = True
        wa = [c1.sb("wa", [128, 8, 256], BF16) for _ in range(2)]
        wpb = [c1.sb("wpb", [128, 8, 256], BF16) for _ in range(2)]
        gat = [c1.sb("gat", [128, 512], F32) for _ in range(2)]
        gpt = [c1.sb("gpt", [128, 512], F32) for _ in range(2)]
        t1 = [c1.sb("t1", [128, 512], F32) for _ in range(2)]
        t2 = [c1.sb("t2", [128, 512], F32) for _ in range(2)]
        for c4 in range(8):
            wa_ = wload(c1, None, dr["w_br_attn"], l * 1024 * D + c4 * 256, D, 8, 256, dst=wa[c4 % 2])
            wp_ = wload(c1, None, dr["w_br_pool"], l * 1024 * D + c4 * 256, D, 8, 256, dst=wpb[c4 % 2])
            for o4 in range(2):
                oc = c4 * 2 + o4
                k2 = oc % 2
                P.dma("sync", lambda e, oc=oc, k2=k2, t0=t0: e.dma_start(out=gat[k2].t[:], in_=gT.ap[oc * 128:(oc + 1) * 128, t0:t0 + 512]), gat[k2].b, reads=[gT.b], writes=[gat[k2].b])
                P.dma("sync", lambda e, oc=oc, k2=k2, t0=t0: e.dma_start(out=gpt[k2].t[:], in_=gT.ap[2048 + oc * 128:2048 + (oc + 1) * 128, t0:t0 + 512]), gpt[k2].b, reads=[gT.b], writes=[gpt[k2].b])
                pa = nps()
                for kc in range(8):
                    P.op("tensor", lambda e, pa=pa, kc=kc, o4=o4, wa_=wa_: e.matmul(pa.t[:], lhsT=wa_.t[:, kc, o4 * 128:(o4 + 1) * 128], rhs=aTb.t[:, kc, :], start=(kc == 0), stop=(kc == 7)),
                         reads=[wa_.b, aTb.b], writes=[pa.b])
                pq = nps()
                for kc in range(8):
                    P.op("tensor", lambda e, pq=pq, kc=kc, o4=o4, wp_=wp_: e.matmul(pq.t[:], lhsT=wp_.t[:, kc, o4 * 128:(o4 + 1) * 128], rhs=pTb.t[:, kc, :], start=(kc == 0), stop=(kc == 7)),
                         reads=[wp_.b, pTb.b], writes=[pq.b])
                P.op("vector", lambda e, pa=pa, k2=k2: e.tensor_tensor(out=t1[k2].t[:], in0=pa.t[:], in1=gat[k2].t[:], op=ALU.mult), reads=[pa.b, gat[k2].b], writes=[t1[k2].b])
                P.op("vector", lambda e, pq=pq, k2=k2: e.tensor_tensor(out=t2[k2].t[:], in0=pq.t[:], in1=gpt[k2].t[:], op=ALU.mult), reads=[pq.b, gpt[k2].b], writes=[t2[k2].b])
                P.op("gpsimd", lambda e, oc=oc, k2=k2: e.tensor_tensor(out=y1T.t[:, oc, :], in0=t1[k2].t[:], in1=t2[k2].t[:], op=ALU.add), reads=[t1[k2].b, t2[k2].b], writes=[y1T.b])
        wm = [c1.sb("wm", [128, 16, 256], BF16) for _ in range(2)]
        for c4 in range(8):
            wm_ = wload(c1, None, dr["w_mix_out"], l * D * D + c4 * 256, D, 16, 256, dst=wm[c4 % 2])
            for tt in range(4):
                ps = nps()
                for kc in range(16):
                    P.op("tensor", lambda e, ps=ps, kc=kc, tt=tt, wm_=wm_: e.matmul(ps.t[:, 0:256], lhsT=y1T.t[:, kc, tt * 128:(tt + 1) * 128], rhs=wm_.t[:, kc, :], start=(kc == 0), stop=(kc == 15)),
                         reads=[wm_.b, y1T.b], writes=[ps.b])
                P.op("scalar", lambda e, ps=ps, tt=tt, c4=c4: e.copy(out=ysb[tt].t[:, c4 * 256:(c4 + 1) * 256], in_=ps.t[:, 0:256]), reads=[ps.b], writes=[ysb[tt].b])
        for tt in range(4):
            post_norm_residual(c1, ysb[tt], xb.t[:, tt, :], xb, gbc, scr)
        c1.close()
        if C_STOP == 3:
            P.disabled = True
        c2 = Cx(nc, P)
        gbc = load_gain_bc(c2, dr["ln_x_post"], l)
        gfm = load_gain_fm(c2, dr["ln_x_pre"], l)
        nrm = NormT(c2, idb)
        hT = c2.sb("hT", [128, 16, 512], BF16)
        xtl = [T(None, "x") for _ in range(4)]
        for tt in range(4):
            xv = T(None, "xv")
            xv.t = _View(xb.t[:, tt, :])
            xv.b = xb.b
            nrm.run(xv, gfm, hT, tt * 128)
        psc = [c2.ps("psc2", [128, 512], F32) for _ in range(6)]
        kp = [0]
        wq = wload(c2, "wq", dr["w_xq"], l * D * 512, 512, 16, 512)
        wo = wload(c2, "wo", dr["w_xo"], l * 512 * D, D, 4, D)
        qxT = c2.sb("qxT", [128, 4, 512], BF16)
        oxT = c2.sb("oxT", [128, 4, 512], BF16)
        Ex = [c2.sb("Ex", [128, 512], BF16) for _ in range(4)]
        rden = c2.sb("rden", [128, 512], F32)
        for h in range(4):
            ps = nps2 = psc[kp[0] % 6]
            kp[0] += 1
            for kc in range(16):
                P.op("tensor", lambda e, ps=ps, kc=kc, h=h: e.matmul(ps.t[:], lhsT=wq.t[:, kc, h * 128:(h + 1) * 128], rhs=hT.t[:, kc, :], start=(kc == 0), stop=(kc == 15)),
                     reads=[wq.b, hT.b], writes=[ps.b])
            P.op("vector", lambda e, ps=ps, h=h: e.tensor_copy(out=qxT.t[:, h, :], in_=ps.t[:]), reads=[ps.b], writes=[qxT.b])
        for h in range(4):
            es = []
            for mt in range(2):
                ps = psc[kp[0] % 6]
                kp[0] += 1
                E = Ex[(h * 2 + mt) % 4]
                P.op("tensor", lambda e, ps=ps, h=h, mt=mt: e.matmul(ps.t[:], lhsT=KmT.t[:, h, mt * 128:(mt + 1) * 128], rhs=qxT.t[:, h, :], start=True, stop=True),
                     reads=[KmT.b, qxT.b], writes=[ps.b])
                P.op("scalar", lambda e, ps=ps, E=E: e.activation(out=E.t[:], in_=ps.t[:], func=AF.Exp, scale=128 ** -0.5), reads=[ps.b], writes=[E.b])
                es.append(E)
            po = psc[kp[0] % 6]
            kp[0] += 1
            pd = psc[kp[0] % 6]
            kp[0] += 1
            for mt in range(2):
                P.op("tensor", lambda e, po=po, h=h, mt=mt, E=es[mt]: e.matmul(po.t[:], lhsT=Vm.t[:, mt, h * 128:(h + 1) * 128], rhs=E.t[:], start=(mt == 0), stop=(mt == 1)),
                     reads=[Vm.b, E.b], writes=[po.b])
            for mt in range(2):
                P.op("tensor", lambda e, pd=pd, mt=mt, E=es[mt]: e.matmul(pd.t[:], lhsT=ones_b.t[:], rhs=E.t[:], start=(mt == 0), stop=(mt == 1)),
                     reads=[ones_b.b, E.b], writes=[pd.b])
            P.op("vector", lambda e, pd=pd: e.reciprocal(out=rden.t[:], in_=pd.t[:]), reads=[pd.b], writes=[rden.b])
            P.op("vector", lambda e, po=po, h=h: e.tensor_tensor(out=oxT.t[:, h, :], in0=po.t[:], in1=rden.t[:], op=ALU.mult), reads=[po.b, rden.b], writes=[oxT.b])
        for c4 in range(4):
            for tt in range(4):
                ps = psc[kp[0] % 6]
                kp[0] += 1
                for h in range(4):
                    P.op("tensor", lambda e, ps=ps, h=h, tt=tt, c4=c4: e.matmul(ps.t[:], lhsT=oxT.t[:, h, tt * 128:(tt + 1) * 128], rhs=wo.t[:, h, c4 * 512:(c4 + 1) * 512], start=(h == 0), stop=(h == 3)),
                         reads=[wo.b, oxT.b], writes=[ps.b])
                P.op("scalar", lambda e, ps=ps, tt=tt, c4=c4: e.copy(out=ysb[tt].t[:, c4 * 512:(c4 + 1) * 512], in_=ps.t[:]), reads=[ps.b], writes=[ysb[tt].b])
        for tt in range(4):
            post_norm_residual(c2, ysb[tt], xb.t[:, tt, :], xb, gbc, scr)
        c2.close()
        if C_STOP == 4:
            P.disabled = True
        c3 = Cx(nc, P)
        gbc = load_gain_bc(c3, dr["ln_ffn_post"], l)
        hT = c3.sb("hT", [128, 16, 512], BF16)
        hid = c3.sb("hid", [128, NFF, 512], BF16)
        cn = Cx(nc, P)
        gfm = load_gain_fm(cn, dr["ln_ffn_pre"], l)
        nrm = NormT(cn, idb)
        for tt in range(4):
            xv = T(None, "xv")
            xv.t = _View(xb.t[:, tt, :])
            xv.b = xb.b
            nrm.run(xv, gfm, hT, tt * 128)
        cn.close()
        ca = Cx(nc, P)
        wg = [ca.sb("wg", [128, 16, 128], BF16) for _ in range(3)]
        wu = [ca.sb("wu", [128, 16, 128], BF16) for _ in range(3)]
        pg = [ca.ps("pg", [128, 512], F32) for _ in range(2)]
        pu = [ca.ps("pu", [128, 512], F32) for _ in range(2)]
        sil = [ca.sb("sil", [128, 512], F32) for _ in range(2)]
        for f in range(NFF):
            wg_ = wload(ca, None, dr["w_gate"], l * D * D_FF + f * 128, D_FF, 16, 128, dst=wg[f % 3])
            wu_ = wload(ca, None, dr["w_up"], l * D * D_FF + f * 128, D_FF, 16, 128, dst=wu[f % 3])
            g_, u_ = pg[f % 2], pu[f % 2]
            for kc in range(16):
                P.op("tensor", lambda e, g_=g_, kc=kc, wg_=wg_: e.matmul(g_.t[:], lhsT=wg_.t[:, kc, :], rhs=hT.t[:, kc, :], start=(kc == 0), stop=(kc == 15)), reads=[wg_.b, hT.b], writes=[g_.b])
            for kc in range(16):
                P.op("tensor", lambda e, u_=u_, kc=kc, wu_=wu_: e.matmul(u_.t[:], lhsT=wu_.t[:, kc, :], rhs=hT.t[:, kc, :], start=(kc == 0), stop=(kc == 15)), reads=[wu_.b, hT.b], writes=[u_.b])
            s_ = sil[f % 2]
            P.op("scalar", lambda e, g_=g_, s_=s_: e.activation(out=s_.t[:], in_=g_.t[:], func=AF.Silu), reads=[g_.b], writes=[s_.b])
            P.op("vector", lambda e, u_=u_, s_=s_, f=f: e.tensor_tensor(out=hid.t[:, f, :], in0=u_.t[:], in1=s_.t[:], op=ALU.mult), reads=[u_.b, s_.b], writes=[hid.b])
        ca.close()
        if C_STOP == 5:
            P.disabled = True
        cb = Cx(nc, P)
        wd = [cb.sb("wd", [128, 4, 512], BF16) for _ in range(3)]
        py = [cb.ps("py", [128, 512], F32) for _ in range(4)]
        kw_ = 0
        for q4 in range(4):
            for f4 in range(NFF // 4):
                wd_ = wd[kw_ % 3]
                kw_ += 1
                src = dr["w_down"].view(l * D_FF * D + f4 * 512 * D + q4 * 512, [[D, 128], [128 * D, 4], [1, 512]])
                P.dma("gpsimd", lambda e, wd_=wd_, src=src: e.dma_start(out=wd_.t[:], in_=src), wd_.b, reads=[dr["w_down"].b], writes=[wd_.b])
                for fi in range(4):
                    f = f4 * 4 + fi
                    for tt in range(4):
                        ps = py[tt]
                        P.op("tensor", lambda e, ps=ps, f=f, fi=fi, tt=tt, wd_=wd_: e.matmul(ps.t[:], lhsT=hid.t[:, f, tt * 128:(tt + 1) * 128], rhs=wd_.t[:, fi, :],
                                                                                       start=(f == 0), stop=(f == NFF - 1)), reads=[wd_.b, hid.b], writes=[ps.b])
            for tt in range(4):
                ps = py[tt]
                P.op("scalar", lambda e, ps=ps, tt=tt, q4=q4: e.copy(out=ysb[tt].t[:, q4 * 512:(q4 + 1) * 512], in_=ps.t[:]), reads=[ps.b], writes=[ysb[tt].b])
        for tt in range(4):
            post_norm_residual(cb, ysb[tt], xb.t[:, tt, :], xb, gbc, scr)
        cb.close()
        c3.close()
        P.disabled = False
        for tt in range(4):
            P.dma("sync", lambda e, tt=tt, t0=t0: e.dma_start(out=xo.ap[t0 + tt * 128:t0 + (tt + 1) * 128, :], in_=xb.t[:, tt, :]), xb.b, reads=[xb.b], writes=[xo.b])
    cxo.close()


class _View:
    def __init__(self, ap):
        self.ap = ap

    def __getitem__(self, key):
        return self.ap[key]


NT_FULL = 16
DEPTH = 4
_PROGS = {}


def _mk(nc, dr, name, shape, dt, kind="ExternalInput"):
    dr[name] = DT(nc, name, shape, dt, kind)


def declare_A(nc, dr, NT, L, kind_out):
    NTOK = NT * 128
    _mk(nc, dr, "x", [NTOK, D], F32)
    _mk(nc, dr, "w_in", [L, D, IN_WIDTH], F32)
    _mk(nc, dr, "ln_mix_pre", [L, D], F32)
    _mk(nc, dr, "ident", [128, 128], F32)
    for name, shape, dt in (("qT", [16, 64, NTOK], BF16), ("kT", [3, 4, 64, NTOK], BF16), ("vcT", [4, 64, NTOK], BF16),
                            ("v", [2, NTOK, 256], BF16), ("gl", [NTOK, 48], F32), ("u", [NTOK, 1024], F32), ("gT", [4096, NTOK], F32)):
        _mk(nc, dr, name, shape, dt, kind_out)


def declare_B_tables(nc, dr, NT):
    SEQ = NT * 8 * 128
    n_sel = SEQ // 64
    NNT = SEQ // 2048
    for name, shape in (("jrev", [128, 128]), ("wstrip", [128, 8192]), ("selall", [48, 3072]), ("ovl", [128, NNT, n_sel]),
                        ("keepS", [128, 2 * n_sel]), ("addS", [128, 2 * n_sel]), ("oh_sel", [33, LEN_S]), ("oh_win", [33, LEN_W]), ("oh_cmp", [33, LEN_C])):
        _mk(nc, dr, name, shape, F32)
    _mk(nc, dr, "gd_sel", [16, LEN_S], BF16, "Internal")
    _mk(nc, dr, "gd_win", [16, LEN_W], BF16, "Internal")
    _mk(nc, dr, "gd_cmp", [16, LEN_C], BF16, "Internal")


def declare_B_w(nc, dr, L):
    _mk(nc, dr, "rel_bias", [16, 32], F32)
    _mk(nc, dr, "cmp_pe", [L, 2, 32, 64], F32)
    _mk(nc, dr, "cmp_w1", [L, 2, 2048, 128], F32)
    _mk(nc, dr, "cmp_w2", [L, 2, 128, 64], F32)


def declare_C_w(nc, dr, L):
    _mk(nc, dr, "mem", [256, D], F32)
    for n in ("ln_mix_post", "ln_x_pre", "ln_x_post", "ln_mem", "ln_ffn_pre", "ln_ffn_post"):
        _mk(nc, dr, n, [L, D], F32)
    for n, shape in (("w_pool", [L, 4, 256, 256]), ("pool_scale", [L, 1024]), ("w_br_attn", [L, 1024, D]), ("w_br_pool", [L, 1024, D]),
                     ("w_mix_out", [L, D, D]), ("w_xq", [L, D, 512]), ("w_xkv", [L, D, 1024]), ("w_xo", [L, 512, D]),
                     ("w_gate", [L, D, D_FF]), ("w_up", [L, D, D_FF]), ("w_down", [L, D_FF, D])):
        _mk(nc, dr, n, shape, F32)
    for n, shape in (("poolM", [2, 128, 4, 128]), ("poolHA", [128, 4, 128]), ("poolHB", [16, 4, 128])):
        _mk(nc, dr, n, shape, F32)


def build_prog_A(NT):
    nc = bass.Bass("TRN2", target_bir_lowering=False)
    dr = {}
    declare_A(nc, dr, NT, 1, "ExternalOutput")
    with ExitStack() as st:
        P = Prog(nc, st)
        phase_A(nc, P, NT, 0, dr)
    return nc


def build_prog_B(NT):
    NTOK = NT * 128
    nc = bass.Bass("TRN2", target_bir_lowering=False)
    dr = {}
    _mk(nc, dr, "qT", [16, 64, NTOK], BF16)
    _mk(nc, dr, "gl", [NTOK, 48], F32)
    _mk(nc, dr, "kTg", [8, 3, 4, 64, NTOK], BF16)
    _mk(nc, dr, "vcTg", [8, 4, 64, NTOK], BF16)
    _mk(nc, dr, "vg", [8, 2, NTOK, 256], BF16)
    _mk(nc, dr, "ident", [128, 128], F32)
    declare_B_w(nc, dr, 1)
    declare_B_tables(nc, dr, NT)
    _mk(nc, dr, "aT", [16, 64, NTOK], BF16, "ExternalOutput")
    with ExitStack() as st:
        P = Prog(nc, st)
        cx, env = phase_B(nc, P, NT, 0, dr)
        phase_B2(nc, P, NT, 0, dr, cx, env)
    return nc


def build_prog_C(NT):
    NTOK = NT * 128
    nc = bass.Bass("TRN2", target_bir_lowering=False)
    dr = {}
    _mk(nc, dr, "x", [NTOK, D], F32)
    _mk(nc, dr, "aT", [16, 64, NTOK], BF16)
    _mk(nc, dr, "u", [NTOK, 1024], F32)
    _mk(nc, dr, "uhg", [8, NT, 16, 1024], F32)
    _mk(nc, dr, "gT", [4096, NTOK], F32)
    _mk(nc, dr, "ident", [128, 128], F32)
    declare_C_w(nc, dr, 1)
    _mk(nc, dr, "x_out", [NTOK, D], F32, "ExternalOutput")
    with ExitStack() as st:
        P = Prog(nc, st)
        phase_C(nc, P, NT, 0, dr)
    return nc


C_W_NAMES = ("ln_mix_post", "ln_x_pre", "ln_x_post", "ln_mem", "ln_ffn_pre", "ln_ffn_post", "w_pool", "pool_scale", "w_br_attn", "w_br_pool",
             "w_mix_out", "w_xq", "w_xkv", "w_xo", "w_gate", "w_up", "w_down")


def run_model(inputs, NT, depth):
    f32 = np.float32
    inp = {k: np.ascontiguousarray(np.asarray(v)) for k, v in inputs.items()}
    NTOK = NT * 128
    S = NTOK * 8
    key = ("unfused", NT)
    if key not in _PROGS:
        _PROGS[key] = (build_prog_A(NT), build_prog_B(NT), build_prog_C(NT))
    pA, pB, pC = _PROGS[key]
    loc = [np.concatenate([np.arange((8 * i + c) * 128, (8 * i + c + 1) * 128) for i in range(NT)]) for c in range(8)]
    xg = inp["x"][0]
    x_loc = [np.ascontiguousarray(xg[loc[c]]) for c in range(8)]
    sh = shared_tables(NT)
    ctab = [dict(core_tables(c, NT), **pool_tables(c)) for c in range(8)]
    cores = list(range(8))
    mem = inp["mem"][0]
    for l in range(depth):
        insA = [{"x": x_loc[c], "w_in": inp["w_in"][l:l + 1], "ln_mix_pre": inp["ln_mix_pre"][l:l + 1], "ident": sh["ident"]} for c in cores]
        rA = run_bass_kernel_spmd(pA, insA, core_ids=cores).results
        kTg = np.stack([np.asarray(rA[c]["kT"]) for c in cores])
        vcTg = np.stack([np.asarray(rA[c]["vcT"]) for c in cores])
        vg = np.stack([np.asarray(rA[c]["v"]) for c in cores])
        uhg = np.stack([np.asarray(rA[c]["u"]).reshape(NT, 128, 1024)[:, 112:, :] for c in cores])
        insB = []
        for c in cores:
            m = {"qT": rA[c]["qT"], "gl": rA[c]["gl"], "kTg": kTg, "vcTg": vcTg, "vg": vg, "ident": sh["ident"], "rel_bias": inp["rel_bias"],
                 "cmp_pe": inp["cmp_pe"][l:l + 1], "cmp_w1": inp["cmp_w1"][l:l + 1], "cmp_w2": inp["cmp_w2"][l:l + 1]}
            for k in ("jrev", "wstrip", "selall", "ovl"):
                m[k] = sh[k]
            for k in ("keepS", "addS", "oh_sel", "oh_win", "oh_cmp"):
                m[k] = ctab[c][k]
            insB.append(m)
        rB = run_bass_kernel_spmd(pB, insB, core_ids=cores).results
        insC = []
        for c in cores:
            m = {"x": x_loc[c], "aT": rB[c]["aT"], "u": rA[c]["u"], "uhg": uhg, "gT": rA[c]["gT"], "ident": sh["ident"], "mem": mem,
                 "poolM": ctab[c]["poolM"], "poolHA": ctab[c]["poolHA"], "poolHB": ctab[c]["poolHB"]}
            for k in C_W_NAMES:
                m[k] = inp[k][l:l + 1]
            insC.append(m)
        rC = run_bass_kernel_spmd(pC, insC, core_ids=cores).results
        x_loc = [np.asarray(rC[c]["x_out"]) for c in cores]
    out = np.empty((S, D), f32)
    for c in cores:
        out[loc[c]] = x_loc[c]
    return out[None]


def kernel(**inputs):
    return run_model(inputs, NT_FULL, DEPTH)
```
